# Optimizing a Trainium2 kernel written in Bass

```python
import jax, jax.numpy as jnp
from jax import lax
import numpy as np

D_MODEL = 1024
BATCH = 8
SEQ = 8192
DEPTH = 1

CHUNK = 64

MIX_WIDTH = D_MODEL
CONV_W = MIX_WIDTH // 2
LRU_W = MIX_WIDTH - CONV_W
N_CONV_HEADS = 8
N_LRU_HEADS = 8
CONV_HD = CONV_W // N_CONV_HEADS
LRU_HD = LRU_W // N_LRU_HEADS
SHORT_CONV_K = 3
LRU_CONV_K = 4
LRU_C = 8.0
N_IN = 3 * CONV_W + 2 * LRU_W

PEER_HEADS = 8
PEER_TOPK = 16
N_KEYS = 128
N_EXPERTS = N_KEYS * N_KEYS
PEER_DK = 128
PEER_DK_HALF = PEER_DK // 2
PEER_BLOCK = 128

EPS = 1e-6

kernel_name = "hybrid_conv_rglru_peer_adaln_block"


def rmsnorm(x, g):
    x32 = x.astype(jnp.float32)
    y = x32 * lax.rsqrt(jnp.mean(x32 * x32, axis=-1, keepdims=True) + EPS)
    return y.astype(x.dtype) * g


def head_rmsnorm(y, g, n_heads):
    b, s, w = y.shape
    yh = y.reshape(b, s, n_heads, w // n_heads).astype(jnp.float32)
    yh = yh * lax.rsqrt(jnp.mean(yh * yh, axis=-1, keepdims=True) + EPS)
    return yh.reshape(b, s, w).astype(y.dtype) * g


def modulate(h, shift, scale):
    return h * (1.0 + scale[:, None, :]) + shift[:, None, :]


def causal_dwconv(x, w):
    k_w = w.shape[0]
    s = x.shape[1]
    xp = jnp.pad(x, ((0, 0), (k_w - 1, 0), (0, 0)))
    out = xp[:, 0:s] * w[0]
    for k in range(1, k_w):
        out = out + xp[:, k:k + s] * w[k]
    return out


def rg_lru(xr, w_r, b_r, w_i, b_i, lam):
    b, s, _ = xr.shape
    xh = xr.reshape(b, s, N_LRU_HEADS, LRU_HD)
    r = jax.nn.sigmoid(jnp.einsum('bshi,hij->bshj', xh, w_r) + b_r).astype(jnp.float32)
    i = jax.nn.sigmoid(jnp.einsum('bshi,hij->bshj', xh, w_i) + b_i).astype(jnp.float32)
    log_a = -LRU_C * r * jax.nn.softplus(-lam.astype(jnp.float32))
    a = jnp.exp(log_a)
    u = jnp.sqrt(-jnp.expm1(2.0 * log_a)) * (i * xh.astype(jnp.float32))

    def combine(e1, e2):
        a1, b1 = e1
        a2, b2 = e2
        return a1 * a2, a2 * b1 + b2

    _, h = lax.associative_scan(combine, (a, u), axis=1)
    return h.reshape(b, s, LRU_W).astype(xr.dtype)


def peer(h, w_q, sub_keys, expert_u, expert_v):
    b, s, d = h.shape
    t = b * s
    hb = h.reshape(t // PEER_BLOCK, PEER_BLOCK, d)
    kk = PEER_TOPK * PEER_TOPK

    def block(xb):
        p = xb.shape[0]
        q = (xb @ w_q).reshape(p, PEER_HEADS, 2, PEER_DK_HALF)
        sc = jnp.einsum('phcd,hcnd->phcn', q, sub_keys).astype(jnp.float32)
        top_s, top_i = lax.top_k(sc, PEER_TOPK)
        cand_s = (top_s[:, :, 0, :, None] + top_s[:, :, 1, None, :]).reshape(p, PEER_HEADS, kk)
        cand_i = (top_i[:, :, 0, :, None] * N_KEYS + top_i[:, :, 1, None, :]).reshape(p, PEER_HEADS, kk)
        best_s, best_pos = lax.top_k(cand_s, PEER_TOPK)
        idx = jnp.take_along_axis(cand_i, best_pos, axis=-1)
        g = jax.nn.softmax(best_s, axis=-1)
        u_sel = expert_u[idx]
        act = jax.nn.gelu(jnp.einsum('phkd,pd->phk', u_sel, xb), approximate=False)
        coef = (g * act.astype(jnp.float32)).astype(xb.dtype)
        return jnp.einsum('phk,phkd->pd', coef, expert_v[idx])

    return lax.map(block, hb).reshape(b, s, d)


def setup_inputs(seed: int = 0) -> dict:
    key = jax.random.key(seed)
    ks = jax.random.split(key, 24)
    f32 = jnp.float32
    L = DEPTH

    def nrm(k, shape, scale):
        return jax.random.normal(k, shape, f32) * scale

    a8 = jax.random.uniform(ks[13], (L, N_LRU_HEADS, LRU_HD), f32, 0.9, 0.999)
    a_base = a8 ** (1.0 / LRU_C)
    lru_lambda = jnp.log(a_base) - jnp.log1p(-a_base)

    return {
        "x": nrm(ks[0], (BATCH, SEQ, D_MODEL), 1.0),
        "c": nrm(ks[1], (BATCH, D_MODEL), 1.0),
        "w_ada": nrm(ks[2], (L, D_MODEL, 6 * D_MODEL), 0.5 * D_MODEL ** -0.5),
        "b_ada": nrm(ks[3], (L, 6 * D_MODEL), 0.02),
        "norm1_g": 1.0 + nrm(ks[4], (L, D_MODEL), 0.02),
        "w_in": nrm(ks[5], (L, D_MODEL, N_IN), D_MODEL ** -0.5),
        "conv_a_w": nrm(ks[6], (L, SHORT_CONV_K, CONV_W), SHORT_CONV_K ** -0.5),
        "conv_b_w": nrm(ks[7], (L, LRU_CONV_K, LRU_W), LRU_CONV_K ** -0.5),
        "conv_b_b": nrm(ks[8], (L, LRU_W), 0.02),
        "w_r": nrm(ks[9], (L, N_LRU_HEADS, LRU_HD, LRU_HD), LRU_HD ** -0.5),
        "b_r": nrm(ks[10], (L, N_LRU_HEADS, LRU_HD), 0.02),
        "w_i": nrm(ks[11], (L, N_LRU_HEADS, LRU_HD, LRU_HD), LRU_HD ** -0.5),
        "b_i": nrm(ks[12], (L, N_LRU_HEADS, LRU_HD), 0.02),
        "lru_lambda": lru_lambda,
        "gn_a": 1.0 + nrm(ks[14], (L, CONV_W), 0.02),
        "gn_b": 1.0 + nrm(ks[15], (L, LRU_W), 0.02),
        "w_out": nrm(ks[16], (L, MIX_WIDTH, D_MODEL), MIX_WIDTH ** -0.5),
        "norm2_g": 1.0 + nrm(ks[17], (L, D_MODEL), 0.02),
        "w_q": nrm(ks[18], (L, D_MODEL, PEER_HEADS * PEER_DK), D_MODEL ** -0.5),
        "sub_keys": nrm(ks[19], (L, PEER_HEADS, 2, N_KEYS, PEER_DK_HALF), PEER_DK_HALF ** -0.5),
        "expert_u": nrm(ks[20], (L, N_EXPERTS, D_MODEL), D_MODEL ** -0.5),
        "expert_v": nrm(ks[21], (L, N_EXPERTS, D_MODEL), PEER_HEADS ** -0.5),
        "final_g": 1.0 + nrm(ks[22], (D_MODEL,), 0.02),
    }


def reference(x, c, w_ada, b_ada, norm1_g, w_in, conv_a_w, conv_b_w, conv_b_b,
              w_r, b_r, w_i, b_i, lru_lambda, gn_a, gn_b, w_out,
              norm2_g, w_q, sub_keys, expert_u, expert_v, final_g):
    c_act = jax.nn.silu(c)
    for l in range(DEPTH):
        ada = c_act @ w_ada[l] + b_ada[l]
        sh1, sc1, g1, sh2, sc2, g2 = jnp.split(ada, 6, axis=-1)

        h = modulate(rmsnorm(x, norm1_g[l]), sh1, sc1)
        z = h @ w_in[l]
        gate_b, gate_c, xa, xr, gr = jnp.split(
            z, [CONV_W, 2 * CONV_W, 3 * CONV_W, 3 * CONV_W + LRU_W], axis=-1)

        y_a = gate_b * causal_dwconv(gate_c * xa, conv_a_w[l])

        xr = causal_dwconv(xr, conv_b_w[l]) + conv_b_b[l]
        y_b = rg_lru(xr, w_r[l], b_r[l], w_i[l], b_i[l], lru_lambda[l]) * jax.nn.gelu(gr)

        y = jnp.concatenate([head_rmsnorm(y_a, gn_a[l], N_CONV_HEADS),
                             head_rmsnorm(y_b, gn_b[l], N_LRU_HEADS)], axis=-1) @ w_out[l]
        x = x + g1[:, None, :] * y

        h = modulate(rmsnorm(x, norm2_g[l]), sh2, sc2)
        x = x + g2[:, None, :] * peer(h, w_q[l], sub_keys[l], expert_u[l], expert_v[l])

    return rmsnorm(x, final_g)
```

```python
import contextlib
import numpy as np
import concourse.bass as bass
import concourse.mybir as mybir
from concourse.bass_utils import run_bass_kernel_spmd

F32 = mybir.dt.float32
BF16 = mybir.dt.bfloat16
I32 = mybir.dt.int32
U32 = mybir.dt.uint32
ALU = mybir.AluOpType
AF = mybir.ActivationFunctionType
AX = mybir.AxisListType

D = 1024
NIN = 2560
NE = 16384
EPS = 1e-6
NU = 7
NVR = 7
NSL = 16
SKEW = 0
NEG = -1.0e30
GAP_DVE = 2
GAP = 1
UV_KIND = "Internal"
ENG = ["pe", "act", "dve", "pool", "sp"]

V_WA = 0
V_WB = 12
V_CBB = 28
V_BR = 32
V_BI = 36
V_LAM = 40
V_GNA = 44
V_GNB = 48
NV = 52


class Buf:
    def __init__(self, name, const=False):
        self.name = name
        self.w = None
        self.r = []
        self.const = const


class Plan:
    def __init__(self):
        self.ops = {e: [] for e in ENG}
        self.cnt = {e: 0 for e in ENG}
        self.waited = {e: {} for e in ENG}
        self.dsem = {}
        self.defer = None

    def handoff(self, srcs, dsts):
        if self.defer is not None:
            self.defer.append(("handoff", list(srcs), list(dsts)))
            return
        handoff(srcs, dsts)

    def commit(self, rec):
        assert self.defer is None
        if rec[0] == "op":
            self.op(*rec[1:])
        elif rec[0] == "dma":
            self.dma(*rec[1:])
        else:
            handoff(rec[1], rec[2])

    def _deps(self, e, reads, writes, extra, skip_self):
        toks = list(extra)
        for b in reads:
            toks.append(b.w)
        for b in writes:
            toks.append(b.w)
            toks.extend(b.r)
        best = {}
        for t in toks:
            if t is None:
                continue
            k, v = t
            if skip_self and k == e:
                continue
            if best.get(k, 0) < v:
                best[k] = v
        waits = []
        for k, v in best.items():
            if self.waited[e].get(k, 0) >= v:
                continue
            self.waited[e][k] = v
            waits.append((k, v))
        return waits

    def _upd(self, tok, reads, writes):
        for b in reads:
            if not b.const:
                b.r.append(tok)
        for b in writes:
            b.w = tok
            b.r = []

    def op(self, e, fn, reads=(), writes=(), extra=(), skip_self=False):
        if self.defer is not None:
            self.defer.append(("op", e, fn, tuple(reads), tuple(writes), tuple(extra), skip_self))
            return None
        waits = self._deps(e, reads, writes, extra, skip_self)
        self.cnt[e] += 1
        tok = (e, self.cnt[e])
        self.ops[e].append((waits, fn, (e, 1)))
        self._upd(tok, reads, writes)
        return tok

    def dma(self, q, fn, sem, reads=(), writes=(), extra=()):
        if self.defer is not None:
            self.defer.append(("dma", q, fn, sem, tuple(reads), tuple(writes), tuple(extra)))
            return None
        waits = self._deps(q, reads, writes, extra, False)
        self.dsem[sem] = self.dsem.get(sem, 0) + 16
        tok = (sem, self.dsem[sem])
        self.ops[q].append((waits, fn, (sem, 16)))
        self._upd(tok, reads, writes)
        return tok

    def wait_only(self, e, toks):
        waits = self._deps(e, (), (), toks, False)
        if waits:
            self.ops[e].append((waits, None, None))


def schedule(recs, gap=2, gap_dma=4, cap=None):
    cap = cap or {"dve": 2, "act": 3, "pe": 3, "pool": 2, "sp": 4}
    lastw, readers = {}, {}
    slot, eng = [], []
    load = {}
    for i, rec in enumerate(recs):
        if rec[0] == "op":
            e, reads, writes = rec[1], rec[3], rec[4]
        elif rec[0] == "dma":
            e, reads, writes = rec[1], rec[4], rec[5]
        else:
            e, reads, writes = None, rec[1], rec[2]
        deps = set()
        for b in reads:
            if id(b) in lastw:
                deps.add(lastw[id(b)])
        for b in writes:
            if id(b) in lastw:
                deps.add(lastw[id(b)])
            deps.update(readers.get(id(b), ()))
        s0 = 0
        for d in deps:
            if eng[d] == e or eng[d] is None or e is None:
                g = 0
            else:
                g = gap_dma if recs[d][0] == "dma" else ((4 if eng[d] == "act" else GAP_DVE) if e == "dve" else gap)
            s0 = max(s0, slot[d] + g)
        if e is not None:
            while load.get((e, s0), 0) >= cap[e]:
                s0 += 1
            load[(e, s0)] = load.get((e, s0), 0) + 1
        slot.append(s0)
        eng.append(e)
        for b in reads:
            readers.setdefault(id(b), []).append(i)
        for b in writes:
            lastw[id(b)] = i
            readers[id(b)] = []
    return slot


def handoff(srcs, dsts):
    toks = []
    for b in srcs:
        if b.w is not None:
            toks.append(b.w)
        toks.extend(b.r)
    for d in dsts:
        d.r.extend(toks)


def build(NT, dbg_tile=None, stage=99):
    S = NT * 128
    nc = bass.Bass("TRN2", target_bir_lowering=False)
    P = Plan()
    es = contextlib.ExitStack()

    def din(name, shape, dt=F32):
        return nc.dram_tensor(name, shape, dt, kind="ExternalInput").ap()

    x_d = din("x", [S, D])
    cin_d = din("cin", [128, 8])
    wada_d = din("w_ada", [128, 8, 6 * D])
    bada_d = din("b_ada", [1, 6 * D])
    grow_d = din("grow", [1, 3 * D])
    win_d = din("w_in", [128, 8, NIN])
    wout_d = din("w_out", [128, 8, D])
    wq_d = din("w_q", [128, 8, D])
    vecs_d = din("vecs", [128, NV])
    wrbd_d = din("wr_bd", [128, 4, 128])
    wibd_d = din("wi_bd", [128, 4, 128])
    skT_d = din("skT", [128, 2, 8, 128])
    eu_d = din("expert_u", [NE, D])
    ev_d = din("expert_v", [NE, D])
    out_d = nc.dram_tensor("out", [S, D], F32, kind="ExternalOutput").ap()
    uv_d = nc.dram_tensor("uvtab", [NE, 2 * D], BF16, kind=UV_KIND).ap()

    dbg_out = {}

    def sb(name, shape, dt):
        return es.enter_context(nc.sbuf_tensor(name, shape, dt))

    w_in_bf = sb("w_in_bf", [128, 8, NIN], BF16)
    w_out_bf = sb("w_out_bf", [128, 8, D], BF16)
    w_q_bf = sb("w_q_bf", [128, 8, D], BF16)
    skT = sb("skT_sb", [128, 2, 8, 128], F32)
    wr_bf = sb("wr_bf", [128, 4, 128], BF16)
    wi_bf = sb("wi_bf", [128, 4, 128], BF16)
    identf = sb("identf", [128, 128], F32)
    ident_bf = sb("ident_bf", [128, 128], BF16)
    blockones = sb("blockones", [128, 128], F32)
    ones_row = sb("ones_row", [1, 128], F32)
    vecs = sb("vecs_sb", [128, NV], F32)
    nsp = sb("nsp", [128, 8], F32)
    iota16 = sb("iota16", [128, 16], F32)
    gmod1_bc = sb("gmod1_bc", [128, D], F32)
    sh1_bc = sb("sh1_bc", [128, D], F32)
    g1_bc = sb("g1_bc", [128, D], F32)
    gmod2_bc = sb("gmod2_bc", [128, D], F32)
    sh2_bc = sb("sh2_bc", [128, D], F32)
    g2_bc = sb("g2_bc", [128, D], F32)
    fg_bc = sb("fg_bc", [128, D], F32)
    CONSTS = Buf("consts")

    xt = [sb("xt0", [128, D], F32)] * 2
    XT = [Buf("xt0")] * 2
    x1s = [sb(f"x1_{i}", [128, D], F32) for i in range(2)]; X1s = [Buf(f"x1_{i}") for i in range(2)]
    ot = sb("ot", [128, D], F32); OT = Buf("ot")
    h = sb("h", [128, D], BF16); H = Buf("h")
    hT = sb("hT", [128, 8, 128], BF16); HT = Buf("hT")
    h2bs = [sb(f"h2b{i}", [128, D], BF16) for i in range(2)]; H2Bs = [Buf(f"h2b{i}") for i in range(2)]
    h2T = sb("h2T", [128, 8, 128], BF16); H2T = Buf("h2T")
    junkD = sb("junkD", [128, D], BF16); JUNKD = Buf("junkD")
    stat = sb("stat", [128, 16], F32)
    STAT = [Buf(f"stat{i}") for i in range(3)]
    mbuf = sb("mbuf", [128, 4, 130], F32); MBUF = Buf("mbuf")
    xrbuf = sb("xrbuf", [128, 4, 131], F32); XRBUF = Buf("xrbuf")
    xcb = sb("xcb", [128, 4, 128], BF16); XCB = Buf("xcb")
    ynT = sb("ynT", [128, 8, 128], BF16)
    YNT = [Buf("ynTa"), Buf("ynTb")]
    hstate = sb("hstate", [128, 4], F32); HST = Buf("hstate")

    uni = sb("uni", [128, 5120], F32)

    def uview(off, n, shape3=None):
        v = uni[:, off:off + n]
        return v

    gc_sb = uni[:, 0:512]; GC = Buf("gc")
    ca = uni[:, 512:1024]; CA = [Buf(f"ca{c}") for c in range(4)]
    sq = uni[:, 1024:1536]; SQ = Buf("sq")
    rstd = uni[:, 1536:2048]; RSTD = Buf("rstd")
    xc = uni[:, 2048:2560]; XC = [Buf(f"xc{c}") for c in range(4)]
    rr = uni[:, 2560:3072]; RR = Buf("rr")
    ii_ = uni[:, 3072:3584]; II = Buf("ii")
    aa = uni[:, 3584:4096]; AA = Buf("aa")
    hs = uni[:, 4096:4608]; HS = Buf("hs")
    gg = uni[:, 4608:5120]; GG = Buf("gg")
    MIXB = [GC, SQ, RSTD, RR, II, AA, HS, GG] + CA + XC
    cand = uni[:, 0:2048]; CAND = Buf("cand")
    oh = uni[:, 2048:3072].bitcast(BF16); OH = Buf("oh")
    prod = uni[:, 3072:4096].bitcast(BF16); PROD = Buf("prod")
    qT = uni[:, 4096:5120]; QT = Buf("qT")
    PEERB = [CAND, OH, PROD, QT]

    top_s = sb("top_s", [128, 8, 2, 16], F32)
    top_i = sb("top_i", [128, 8, 2, 16], U32)
    TOP = [[Buf(f"top{hd}_{c}") for c in range(2)] for hd in range(8)]
    top_if = sb("top_if", [128, 8, 2, 16], F32); TOPIF = Buf("top_if")
    work = uni[:, 3072:3328].rearrange("p (a b) -> p a b", a=2); WORK = [PROD, PROD]
    work2 = uni[:, 2048:2560].rearrange("p (a b) -> p a b", a=2); WORK2 = [OH, OH]
    best_s = sb("best_s", [128, 8, 16], F32)
    best_pos = sb("best_pos", [128, 8, 16], U32)
    BEST = [Buf(f"best{hd}") for hd in range(8)]
    iju = sb("iju", [128, 2, 128], U32); IJU = Buf("iju")
    ijf = sb("ijf", [128, 2, 128], F32); IJF = Buf("ijf")
    i12 = sb("i12", [128, 2, 128], F32); I12 = Buf("i12")
    idxf = sb("idxf", [128, 128], F32); IDXF = Buf("idxf")
    idx32s = [sb(f"idx32_{i}", [128, 128], I32) for i in range(2)]; IDXs = [Buf(f"idx32_{i}") for i in range(2)]
    ev_ = sb("esm", [128, 8, 16], F32); ESM = Buf("esm")
    gsms = [sb(f"gsm{i}", [128, 8, 16], F32) for i in range(2)]; GSMs = [Buf(f"gsm{i}") for i in range(2)]
    ssum = sb("ssum", [128, 16], F32); SSUM = Buf("ssum")
    actv = sb("actv", [128, 128], F32)
    gev = sb("gev", [128, 128], F32)
    coefv = sb("coefv", [128, 128], F32)
    SLOTB = [Buf(f"slotb{i}") for i in range(NSL)]
    diag = sb("diag", [128, 4, 128], BF16)
    DIAG = [Buf(f"diag{i}") for i in range(4)]
    ringt = sb("ring", [128, (NU + NVR) * D], BF16)
    RING = [Buf(f"stg{j}") for j in range(6)]
    RU = [Buf(f"ru{j}") for j in range(NU)]
    RV = [Buf(f"rv{j}") for j in range(NVR)]
    ring = ringt[:, 0:6 * 2 * D].rearrange("p (a b) -> p a b", a=6)
    ringf = ringt[:].bitcast(F32)

    def ru(j):
        return ringt[:, 2 * j * D:(2 * j + 1) * D]

    def rv(j):
        return ringt[:, (2 * j + 1) * D:(2 * j + 2) * D]

    def ruv(ju, jv):
        assert ju == jv
        return ringt[:, 2 * ju * D:(2 * ju + 2) * D]
    small = uni[0:1, 0:1536].rearrange("p (a b) -> p a b", a=3); SMALL = Buf("small")

    ps = es.enter_context(nc.psum_tensor("ps", [128, 8, 512], F32))
    PB = [Buf(f"psb{b}") for b in range(8)]
    trb = ps[:, 0, :].bitcast(BF16).rearrange("p (c t) -> p c t", c=8)

    WIN = Buf("w_in_bf"); WIN2 = Buf("w_in_bf2"); win_tokens = []; WOUT = Buf("w_out_bf"); WQ = Buf("w_q_bf")

    def act_fn(out, in_, func, bias=None, scale=None, accum=None):
        kw = {}
        if bias is not None:
            kw["bias"] = bias
        if scale is not None:
            kw["scale"] = scale
        if accum is not None:
            kw["accum_out"] = accum
        return lambda e: e.activation(out, in_, func, **kw)

    def dump(name, ap, shape, dt, buf):
        if dbg_tile is None:
            return
        dd = nc.dram_tensor("dbg_" + name, list(shape), dt, kind="ExternalOutput").ap()
        dbg_out[name] = True
        P.dma("sp", lambda e: e.dma_start(out=dd, in_=ap), "dbg_" + name, reads=[buf])

    RALL = RING

    onesf = ringf[:, 0:128]
    P.op("pool", lambda e: e.memset(onesf, 1.0), writes=RALL)
    P.op("pool", lambda e: e.affine_select(out=identf[:], in_=onesf, pattern=[[-1, 128]],
                                           compare_op=ALU.is_equal, fill=0.0, base=0,
                                           channel_multiplier=1),
         reads=RALL, writes=[CONSTS])
    P.op("pool", lambda e: e.tensor_copy(out=ident_bf[:], in_=identf[:]), reads=[CONSTS], writes=[CONSTS])

    P.op("pool", lambda e: e.memset(blockones[:], 0.0), writes=[CONSTS])
    P.op("pool", lambda e: e.memset(blockones[0:64, 0:64], 1.0 / 64), writes=[CONSTS])
    P.op("pool", lambda e: e.memset(blockones[64:128, 64:128], 1.0 / 64), writes=[CONSTS])
    P.op("pool", lambda e: e.memset(ones_row[:], 1.0), writes=[CONSTS])
    P.op("pool", lambda e: e.iota(iota16[:], pattern=[[1, 16]], base=0, channel_multiplier=0,
                                  allow_small_or_imprecise_dtypes=True), writes=[CONSTS])
    P.op("pool", lambda e: e.memset(mbuf[:], 0.0), writes=[MBUF])
    P.op("pool", lambda e: e.memset(xrbuf[:], 0.0), writes=[XRBUF])
    P.op("pool", lambda e: e.memset(hstate[:], 0.0), writes=[HST])

    VEC = Buf("vecs")
    P.dma("sp", lambda e: e.dma_start(out=vecs[:], in_=vecs_d[:, :]), "ld_vecs", writes=[VEC])
    SKT = Buf("skT")
    P.dma("sp", lambda e: e.dma_start(out=skT[:], in_=skT_d[:, :, :, :]), "ld_skT", writes=[SKT])
    cs = stat[:, 8:16]
    CS = Buf("cs")
    P.dma("sp", lambda e: e.dma_start(out=cs, in_=cin_d[:, :]), "ld_cs", writes=[CS])
    P.op("act", act_fn(cs, cs, AF.Silu), reads=[CS], writes=[CS])

    stg = ringf
    P.dma("sp", lambda e: e.dma_start(out=stg[:, 0:512], in_=wrbd_d[:, :, :].rearrange("p a b -> p (a b)")),
          "ld_stg", writes=RALL)
    P.op("dve", lambda e: e.tensor_copy(out=wr_bf[:].rearrange("p a b -> p (a b)"), in_=stg[:, 0:512]),
         reads=RALL, writes=[CONSTS])
    P.dma("sp", lambda e: e.dma_start(out=stg[:, 0:512], in_=wibd_d[:, :, :].rearrange("p a b -> p (a b)")),
          "ld_stg", writes=RALL)
    P.op("dve", lambda e: e.tensor_copy(out=wi_bf[:].rearrange("p a b -> p (a b)"), in_=stg[:, 0:512]),
         reads=RALL, writes=[CONSTS])

    def nsp_ops():
        lam = vecs[:, V_LAM:V_LAM + 4]
        e_ = stat[:, 0:4]
        t1 = stat[:, 4:8]
        NS = Buf("nsp_tmp")
        P.op("act", act_fn(e_, lam, AF.Exp, scale=-1.0), reads=[VEC], writes=[NS])
        P.op("dve", lambda e: e.tensor_scalar(out=t1, in0=e_, scalar1=-0.2, scalar2=0.25, op0=ALU.mult, op1=ALU.add),
             reads=[NS], writes=[NS])
        for cst in (1.0 / 3, 0.5, 1.0):
            P.op("dve", lambda e: e.tensor_tensor(out=t1, in0=t1, in1=e_, op=ALU.mult), reads=[NS], writes=[NS])
            P.op("dve", lambda e, cst=cst: e.tensor_scalar(out=t1, in0=t1, scalar1=-1.0, scalar2=cst,
                                                          op0=ALU.mult, op1=ALU.add), reads=[NS], writes=[NS])
        P.op("dve", lambda e: e.tensor_tensor(out=t1, in0=t1, in1=e_, op=ALU.mult), reads=[NS], writes=[NS])
        l1 = nsp[:, 4:8]
        P.op("act", act_fn(l1, e_, AF.Ln, bias=1.0), reads=[NS], writes=[NS])
        msk = nsp[:, 0:4]
        P.op("dve", lambda e: e.tensor_single_scalar(out=msk, in_=e_, scalar=0.05, op=ALU.is_lt), reads=[NS], writes=[NS])
        P.op("dve", lambda e: e.tensor_tensor(out=t1, in0=t1, in1=l1, op=ALU.subtract), reads=[NS], writes=[NS])
        P.op("dve", lambda e: e.tensor_tensor(out=t1, in0=t1, in1=msk, op=ALU.mult), reads=[NS], writes=[NS])
        P.op("dve", lambda e: e.tensor_tensor(out=t1, in0=t1, in1=l1, op=ALU.add), reads=[NS], writes=[NS])
        P.op("dve", lambda e: e.tensor_scalar(out=nsp[:, 0:4], in0=t1, scalar1=-8.0, scalar2=None, op0=ALU.mult),
             reads=[NS], writes=[NS])
        P.op("dve", lambda e: e.tensor_scalar(out=nsp[:, 4:8], in0=t1, scalar1=-16.0, scalar2=None, op0=ALU.mult),
             reads=[NS], writes=[CONSTS, NS])
    nsp_ops()

    bc_dst = [sh1_bc, gmod1_bc, g1_bc, sh2_bc, gmod2_bc, g2_bc]
    wst = stg[:, 0:4096].rearrange("p (k n) -> p k n", k=8)
    for j in range(12):
        v, half = j // 2, j % 2
        P.dma("sp", lambda e, j=j: e.dma_start(out=wst, in_=wada_d[:, :, j * 512:(j + 1) * 512]),
              "ld_stg", writes=RALL)
        P.dma("sp", lambda e, j=j: e.dma_start(out=small[0:1, 0, :], in_=bada_d[0:1, j * 512:(j + 1) * 512]),
              "ld_small", writes=[SMALL])
        if v in (1, 4):
            go = (0 if v == 1 else 1) * D + half * 512
            P.dma("sp", lambda e, go=go: e.dma_start(out=small[0:1, 1, :], in_=grow_d[0:1, go:go + 512]),
                  "ld_small", writes=[SMALL])

        def mm_ada(e):
            for kc in range(8):
                r_ = e.matmul(ps[0:1, 6, :], lhsT=cs[:, kc:kc + 1], rhs=wst[:, kc, :], start=(kc == 0), stop=(kc == 7))
            return r_
        P.op("pe", mm_ada, reads=RALL + [CS], writes=[PB[6]])
        row = small[0:1, 2, :]
        P.op("dve", lambda e: e.tensor_tensor(out=row, in0=ps[0:1, 6, :], in1=small[0:1, 0, :], op=ALU.add),
             reads=[PB[6], SMALL], writes=[SMALL])
        if v in (1, 4):
            P.op("dve", lambda e: e.scalar_tensor_tensor(out=row, in0=row, scalar=1.0, in1=small[0:1, 1, :],
                                                         op0=ALU.add, op1=ALU.mult),
                 reads=[SMALL], writes=[SMALL])
        P.op("pe", lambda e: e.matmul(ps[:, 7, :], lhsT=ones_row[0:1, :], rhs=row, start=True, stop=True),
             reads=[SMALL, CONSTS], writes=[PB[7]])
        dst = bc_dst[v]
        P.op("act", lambda e, dst=dst, half=half: e.copy(out=dst[:, half * 512:(half + 1) * 512], in_=ps[:, 7, :]),
             reads=[PB[7]], writes=[CONSTS])
    for half in range(2):
        go = 2 * D + half * 512
        P.dma("sp", lambda e, go=go: e.dma_start(out=small[0:1, 1, :], in_=grow_d[0:1, go:go + 512]),
              "ld_small", writes=[SMALL])
        P.op("pe", lambda e: e.matmul(ps[:, 7, :], lhsT=ones_row[0:1, :], rhs=small[0:1, 1, :], start=True, stop=True),
             reads=[SMALL, CONSTS], writes=[PB[7]])
        P.op("act", lambda e, half=half: e.copy(out=fg_bc[:, half * 512:(half + 1) * 512], in_=ps[:, 7, :]),
             reads=[PB[7]], writes=[CONSTS])

    for pc in range(4):
        v5 = stg[:, 0:5120].rearrange("p (k n) -> p k n", k=2)
        P.dma("sp", lambda e, pc=pc: e.dma_start(out=v5, in_=win_d[:, 2 * pc:2 * pc + 2, :]), "ld_stg", writes=RALL)
        P.op("act", lambda e, pc=pc: e.copy(out=w_in_bf[:, 2 * pc, :], in_=v5[:, 0, :]), reads=RALL, writes=[WIN])
        P.op("dve", lambda e, pc=pc: e.tensor_copy(out=w_in_bf[:, 2 * pc + 1, :], in_=v5[:, 1, :]), reads=RALL, writes=[WIN])
    for (src_d, dst, DB) in ((wout_d, w_out_bf, WOUT), (wq_d, w_q_bf, WQ)):
        for pc in range(2):
            v4 = stg[:, 0:4096].rearrange("p (k n) -> p k n", k=4)
            P.dma("sp", lambda e, pc=pc, src_d=src_d: e.dma_start(out=v4, in_=src_d[:, 4 * pc:4 * pc + 4, :]),
                  "ld_stg", writes=RALL)
            P.op("act", lambda e, pc=pc, dst=dst: e.copy(out=dst[:, 4 * pc:4 * pc + 2, :], in_=v4[:, 0:2, :]),
                 reads=RALL, writes=[DB])
            P.op("dve", lambda e, pc=pc, dst=dst: e.tensor_copy(out=dst[:, 4 * pc + 2:4 * pc + 4, :], in_=v4[:, 2:4, :]),
                 reads=RALL, writes=[DB])

    uv_tokens = []
    NBLK = NE // 128

    def uv_loads(r):
        st_ = r % 2
        sU, sV = RING[3 * st_], RING[3 * st_ + 1]
        fu = ringf[:, (3 * st_) * 1024:(3 * st_ + 1) * 1024]
        fv = ringf[:, (3 * st_ + 1) * 1024:(3 * st_ + 2) * 1024]
        P.dma("sp", lambda e: e.dma_start(out=fu, in_=eu_d[r * 128:(r + 1) * 128, :]), f"stg{3 * st_}", writes=[sU])
        P.dma("sp", lambda e: e.dma_start(out=fv, in_=ev_d[r * 128:(r + 1) * 128, :]), f"stg{3 * st_ + 1}", writes=[sV])

    if stage >= 1:
        uv_loads(0)
    for r in range(NBLK if stage >= 1 else 0):
        if r + 1 < NBLK:
            uv_loads(r + 1)
        st_ = r % 2
        sU, sV, sO = RING[3 * st_], RING[3 * st_ + 1], RING[3 * st_ + 2]
        fu = ringf[:, (3 * st_) * 1024:(3 * st_ + 1) * 1024]
        fv = ringf[:, (3 * st_ + 1) * 1024:(3 * st_ + 2) * 1024]
        o = ring[:, 3 * st_ + 2, :]
        OU = Buf("tmp")
        t1 = P.op("act", lambda e, o=o, fu=fu: e.copy(out=o[:, 0:D], in_=fu), reads=[sU], writes=[sO])
        t2 = P.op("dve", lambda e, o=o, fv=fv: e.tensor_copy(out=o[:, D:2 * D], in_=fv), reads=[sV, sO], writes=[OU])
        tk = P.dma("sp", lambda e, o=o, r=r: e.dma_start(out=uv_d[r * 128:(r + 1) * 128, :], in_=o),
                   f"stg{3 * st_ + 2}", reads=[sO, OU])
        uv_tokens.append(tk)

    for b in (WIN, WOUT, WQ, CONSTS, VEC, SKT):
        b.const = True
    handoff(RING, RU + RV)

    def load_x(t):
        P.dma("sp", lambda e: e.dma_start(out=xt[t % 2][:], in_=x_d[t * 128:(t + 1) * 128, :]),
              f"ld_xt{t % 2}", writes=[XT[t % 2]])

    def rms_stats(src, SRC, k, junk, JUNK):
        ssq = stat[:, k:k + 1]
        P.op("act", act_fn(junk, src, AF.Square, accum=ssq), reads=[SRC], writes=[JUNK, STAT[k]])
        P.op("act", act_fn(ssq, ssq, AF.Sqrt, bias=EPS, scale=1.0 / D), reads=[STAT[k]], writes=[STAT[k]])
        P.op("dve", lambda e: e.reciprocal(out=ssq, in_=ssq), reads=[STAT[k]], writes=[STAT[k]])
        return ssq

    def head_norm(y, YB, gcol, yn_off, YN, bk=2):
        P.op("act", act_fn(sq, y, AF.Square), reads=YB, writes=[SQ])
        P.op("pe", lambda e: e.matmul(ps[:, bk, :], lhsT=blockones[:], rhs=sq, start=True, stop=True),
             reads=[SQ, CONSTS], writes=[PB[bk]])
        P.op("act", act_fn(rstd, ps[:, bk, :], AF.Sqrt, bias=EPS), reads=[PB[bk]], writes=[RSTD])
        P.op("dve", lambda e: e.reciprocal(out=rstd, in_=rstd), reads=[RSTD], writes=[RSTD])

        def f(e):
            for c in range(4):
                r_ = e.scalar_tensor_tensor(out=ynT[:, yn_off + c, :], in0=y[:, c * 128:(c + 1) * 128],
                                            scalar=vecs[:, gcol + c:gcol + c + 1],
                                            in1=rstd[:, c * 128:(c + 1) * 128], op0=ALU.mult, op1=ALU.mult)
            return r_
        P.op("dve", f, reads=list(YB) + [RSTD, VEC], writes=[YN])

    first_w = [True]
    gather_ctr = [0]
    first_gather = [True]

    def finish(t):
        return P.dma("sp", lambda e: e.dma_start(out=out_d[t * 128:(t + 1) * 128, :], in_=ot[:]), "st_out", reads=[OT])

    ZB = [1, 2, 3, 4, 7]

    def pre(t):
        X = XT[t % 2]
        xtt = xt[t % 2]
        x1 = x1s[t % 2]; X1 = X1s[t % 2]
        h2b = h2bs[t % 2]; H2B = H2Bs[t % 2]
        idx32 = idx32s[t % 2]; IDX = IDXs[t % 2]
        gsm = gsms[t % 2]; GSM = GSMs[t % 2]
        load_x(t)
        P.handoff(PEERB, MIXB)
        r1 = rms_stats(xtt[:], X, 0, ot[:], OT)
        P.op("dve", lambda e: e.scalar_tensor_tensor(out=ot[:], in0=xtt[:], scalar=r1, in1=gmod1_bc[:],
                                                     op0=ALU.mult, op1=ALU.mult),
             reads=[X, STAT[0], CONSTS], writes=[OT])
        P.op("dve", lambda e: e.tensor_tensor(out=h[:], in0=ot[:], in1=sh1_bc[:], op=ALU.add),
             reads=[OT, CONSTS], writes=[H])
        if t == dbg_tile:
            dump("h", h[:], [128, D], BF16, H)

        def trh(e, src=h):
            for c in range(8):
                r_ = e.transpose(out=trb[:, c, :], in_=src[:, c * 128:(c + 1) * 128], identity=ident_bf[:])
            return r_
        P.op("pe", trh, reads=[H, CONSTS], writes=[PB[0]])
        P.op("act", lambda e: e.copy(out=hT[:].rearrange("p a b -> p (a b)"), in_=ps[:, 0, :].bitcast(BF16)),
             reads=[PB[0]], writes=[HT])
        for bk in range(5):
            def zmm(e, bk=bk):
                for f4 in range(4):
                    fc = bk * 4 + f4
                    for kc in range(8):
                        r_ = e.matmul(ps[:, ZB[bk], f4 * 128:(f4 + 1) * 128],
                                      lhsT=w_in_bf[:, kc, fc * 128:(fc + 1) * 128], rhs=hT[:, kc, :],
                                      start=(kc == 0), stop=(kc == 7))
                return r_
            P.op("pe", zmm, reads=[HT, WIN], writes=[PB[ZB[bk]]])

        P.op("act", lambda e: e.copy(out=gc_sb, in_=ps[:, 2, :]), reads=[PB[2]], writes=[GC])
        P.op("dve", lambda e: e.tensor_tensor(out=mbuf[:, :, 2:130],
                                              in0=gc_sb.rearrange("p (c t) -> p c t", c=4),
                                              in1=ps[:, 3, :].rearrange("p (c t) -> p c t", c=4), op=ALU.mult),
             reads=[GC, PB[3]], writes=[MBUF])
        for k in (2, 1, 0):
            for c in range(4):
                wcol = vecs[:, V_WA + 4 * k + c:V_WA + 4 * k + c + 1]
                dstc = ca[:, c * 128:(c + 1) * 128]
                if k == 2:
                    P.op("dve", lambda e, c=c, wcol=wcol, dstc=dstc: e.tensor_scalar(
                        out=dstc, in0=mbuf[:, c, 2:130], scalar1=wcol, scalar2=None, op0=ALU.mult),
                        reads=[MBUF, VEC], writes=[CA[c]])
                else:
                    P.op("dve", lambda e, c=c, k=k, wcol=wcol, dstc=dstc: e.scalar_tensor_tensor(
                        out=dstc, in0=mbuf[:, c, k:k + 128], scalar=wcol, in1=dstc, op0=ALU.mult, op1=ALU.add),
                        reads=[MBUF, VEC, CA[c]], writes=[CA[c]])
        P.op("dve", lambda e: e.tensor_copy(out=mbuf[:, :, 0:2], in_=mbuf[:, :, 128:130]), reads=[MBUF], writes=[MBUF])
        P.op("dve", lambda e: e.tensor_tensor(out=ca, in0=ps[:, 1, :], in1=ca, op=ALU.mult),
             reads=[PB[1]] + CA, writes=CA)
        head_norm(ca, CA, V_GNA, 0, YNT[0])

        P.op("act", lambda e: e.copy(out=xrbuf[:, :, 3:131], in_=ps[:, 4, :].rearrange("p (c t) -> p c t", c=4)),
             reads=[PB[4]], writes=[XRBUF])
        for k in (3, 2, 1, 0):
            for c in range(4):
                wcol = vecs[:, V_WB + 4 * k + c:V_WB + 4 * k + c + 1]
                dstc = xc[:, c * 128:(c + 1) * 128]
                if k == 3:
                    bcol = vecs[:, V_CBB + c:V_CBB + c + 1]
                    P.op("dve", lambda e, c=c, wcol=wcol, bcol=bcol, dstc=dstc: e.tensor_scalar(
                        out=dstc, in0=xrbuf[:, c, 3:131], scalar1=wcol, scalar2=bcol, op0=ALU.mult, op1=ALU.add),
                        reads=[XRBUF, VEC], writes=[XC[c]])
                else:
                    P.op("dve", lambda e, c=c, k=k, wcol=wcol, dstc=dstc: e.scalar_tensor_tensor(
                        out=dstc, in0=xrbuf[:, c, k:k + 128], scalar=wcol, in1=dstc, op0=ALU.mult, op1=ALU.add),
                        reads=[XRBUF, VEC, XC[c]], writes=[XC[c]])
        P.op("dve", lambda e: e.tensor_copy(out=xrbuf[:, :, 0:3], in_=xrbuf[:, :, 128:131]), reads=[XRBUF], writes=[XRBUF])
        P.op("act", lambda e: e.copy(out=xcb[:].rearrange("p a b -> p (a b)"), in_=xc), reads=XC, writes=[XCB])

        def gate_mm(e, wbf, bank):
            for c in range(4):
                r_ = e.matmul(ps[:, bank, c * 128:(c + 1) * 128], lhsT=wbf[:, c, :], rhs=xcb[:, c, :], start=True, stop=True)
            return r_
        P.op("pe", lambda e: gate_mm(e, wr_bf, 3), reads=[XCB, CONSTS], writes=[PB[3]])
        P.op("pe", lambda e: gate_mm(e, wi_bf, 4), reads=[XCB, CONSTS], writes=[PB[4]])

        def sig(e, dst, bank, bcol0):
            for c in range(4):
                r_ = e.activation(dst[:, c * 128:(c + 1) * 128], ps[:, bank, c * 128:(c + 1) * 128], AF.Sigmoid,
                                  bias=vecs[:, bcol0 + c:bcol0 + c + 1])
            return r_
        P.op("act", lambda e: sig(e, rr, 3, V_BR), reads=[PB[3], VEC], writes=[RR])
        P.op("act", lambda e: sig(e, ii_, 4, V_BI), reads=[PB[4], VEC], writes=[II])

        def expa(e, dst, col0):
            for c in range(4):
                r_ = e.activation(dst[:, c * 128:(c + 1) * 128], rr[:, c * 128:(c + 1) * 128], AF.Exp,
                                  scale=nsp[:, col0 + c:col0 + c + 1])
            return r_
        P.op("act", lambda e: expa(e, aa, 0), reads=[RR, CONSTS], writes=[AA])
        P.op("act", lambda e: expa(e, rr, 4), reads=[RR, CONSTS], writes=[RR])
        P.op("act", act_fn(rr, rr, AF.Sqrt, bias=1.0, scale=-1.0), reads=[RR], writes=[RR])
        P.op("dve", lambda e: e.tensor_tensor(out=ii_, in0=ii_, in1=xc, op=ALU.mult), reads=[II] + XC, writes=[II])
        P.op("dve", lambda e: e.tensor_tensor(out=rr, in0=rr, in1=ii_, op=ALU.mult), reads=[RR, II], writes=[RR])

        def scan(e):
            for c in range(4):
                r_ = e.tensor_tensor_scan(out=hs[:, c * 128:(c + 1) * 128], data0=aa[:, c * 128:(c + 1) * 128],
                                          data1=rr[:, c * 128:(c + 1) * 128], initial=hstate[:, c:c + 1],
                                          op0=ALU.mult, op1=ALU.add)
            return r_
        P.op("dve", scan, reads=[AA, RR, HST], writes=[HS])
        P.op("dve", lambda e: e.tensor_copy(out=hstate[:].unsqueeze(2),
                                            in_=hs.rearrange("p (c t) -> p c t", c=4)[:, :, 127:128]),
             reads=[HS], writes=[HST])
        P.op("act", act_fn(gg, ps[:, 7, :], AF.Gelu), reads=[PB[7]], writes=[GG])
        P.op("dve", lambda e: e.tensor_tensor(out=hs, in0=hs, in1=gg, op=ALU.mult), reads=[HS, GG], writes=[HS])
        head_norm(hs, [HS], V_GNB, 4, YNT[1])
        if t == dbg_tile:
            dump("ynT", ynT[:], [128, 8, 128], BF16, YNT[1])

        def omm(e):
            for half in range(2):
                for kc in range(8):
                    r_ = e.matmul(ps[:, 1 + half, :], lhsT=ynT[:, kc, :], rhs=w_out_bf[:, kc, half * 512:(half + 1) * 512],
                                  start=(kc == 0), stop=(kc == 7))
            return r_
        P.op("pe", omm, reads=YNT + [WOUT], writes=[PB[1], PB[2]])
        P.op("dve", lambda e: e.tensor_tensor(out=ot[:].rearrange("p (a b) -> p a b", a=2), in0=ps[:, 1:3, :],
                                              in1=g1_bc[:].rearrange("p (a b) -> p a b", a=2), op=ALU.mult),
             reads=[PB[1], PB[2], CONSTS], writes=[OT])
        P.op("dve", lambda e: e.tensor_tensor(out=x1[:], in0=ot[:], in1=xtt[:], op=ALU.add),
             reads=[OT, X], writes=[X1])
        if t == dbg_tile:
            dump("x1", x1[:], [128, D], F32, X1)

        r2 = rms_stats(x1[:], X1, 1, ot[:], OT)
        P.op("dve", lambda e: e.scalar_tensor_tensor(out=ot[:], in0=x1[:], scalar=r2, in1=gmod2_bc[:],
                                                     op0=ALU.mult, op1=ALU.mult),
             reads=[X1, STAT[1], CONSTS], writes=[OT])
        P.op("dve", lambda e: e.tensor_tensor(out=h2b[:], in0=ot[:], in1=sh2_bc[:], op=ALU.add),
             reads=[OT, CONSTS], writes=[H2B])
        P.op("pe", lambda e: trh(e, h2b), reads=[H2B, CONSTS], writes=[PB[0]])
        P.op("act", lambda e: e.copy(out=h2T[:].rearrange("p a b -> p (a b)"), in_=ps[:, 0, :].bitcast(BF16)),
             reads=[PB[0]], writes=[H2T])
        P.handoff(MIXB, PEERB)
        for bk in range(2):
            def qmm(e, bk=bk):
                for f4 in range(4):
                    hd = bk * 4 + f4
                    for kc in range(8):
                        r_ = e.matmul(ps[:, 3 + bk, f4 * 128:(f4 + 1) * 128],
                                      lhsT=w_q_bf[:, kc, hd * 128:(hd + 1) * 128], rhs=h2T[:, kc, :],
                                      start=(kc == 0), stop=(kc == 7))
                return r_
            P.op("pe", qmm, reads=[H2T, WQ], writes=[PB[3 + bk]])
            P.op("act", lambda e, bk=bk: e.copy(out=qT[:, bk * 512:(bk + 1) * 512], in_=ps[:, 3 + bk, :]),
                 reads=[PB[3 + bk]], writes=[QT])
        qT3 = qT.rearrange("p (a b) -> p a b", a=8)
        for pr in range(4):
            bank = 1 + pr % 2

            def smm(e, pr=pr, bank=bank):
                for hh in range(2):
                    hd = 2 * pr + hh
                    for c in range(2):
                        col = (hh * 2 + c) * 128
                        r_ = e.matmul(ps[:, bank, col:col + 128], lhsT=qT3[:, hd, :],
                                      rhs=skT[:, c, hd, :], start=True, stop=True)
                return r_
            P.op("pe", smm, reads=[QT, SKT], writes=[PB[bank]])
            for hh in range(2):
                hd = 2 * pr + hh
                for c in range(2):
                    col = (hh * 2 + c) * 128
                    src = ps[:, bank, col:col + 128]
                    wi_ = (hd * 2 + c) % 2
                    wk = work[:, wi_, :]
                    TB = TOP[hd][c]

                    def f1(e, src=src, hd=hd, c=c):
                        return e.max(out=top_s[:, hd, c, 0:8], in_=src)
                    P.op("dve", f1, reads=[PB[bank]], writes=[TB])

                    def f2(e, src=src, hd=hd, c=c, wk=wk):
                        return e.match_replace(out=wk, in_to_replace=top_s[:, hd, c, 0:8], in_values=src, imm_value=NEG)
                    P.op("dve", f2, reads=[PB[bank], TB], writes=[WORK[wi_]])

                    def f3(e, src=src, hd=hd, c=c):
                        return e.max_index(out=top_i[:, hd, c, 0:8], in_max=top_s[:, hd, c, 0:8], in_values=src)
                    P.op("dve", f3, reads=[PB[bank], TB], writes=[TB])

                    def f4_(e, hd=hd, c=c, wk=wk):
                        return e.max(out=top_s[:, hd, c, 8:16], in_=wk)
                    P.op("dve", f4_, reads=[WORK[wi_], TB], writes=[TB])

                    def f5(e, hd=hd, c=c, wk=wk):
                        return e.max_index(out=top_i[:, hd, c, 8:16], in_max=top_s[:, hd, c, 8:16], in_values=wk)
                    P.op("dve", f5, reads=[WORK[wi_], TB], writes=[TB])
        ALLTOP = [TOP[hd][c] for hd in range(8) for c in range(2)]
        P.op("dve", lambda e: e.tensor_copy(out=top_if[:], in_=top_i[:]), reads=ALLTOP, writes=[TOPIF])
        cand4 = cand.rearrange("p (a b c) -> p a b c", a=8, b=16)
        P.op("dve", lambda e: e.tensor_tensor(
            out=cand4, in0=top_s[:, :, 0, :].unsqueeze(3).broadcast_to([128, 8, 16, 16]),
            in1=top_s[:, :, 1, :].unsqueeze(2).broadcast_to([128, 8, 16, 16]), op=ALU.add),
            reads=ALLTOP, writes=[CAND])
        for hd in range(8):
            ch = cand[:, hd * 256:(hd + 1) * 256]
            w2 = work2[:, hd % 2, :]
            W2 = WORK2[hd % 2]
            BB = BEST[hd]
            P.op("dve", lambda e, hd=hd, ch=ch: e.max(out=best_s[:, hd, 0:8], in_=ch), reads=[CAND], writes=[BB])
            P.op("dve", lambda e, hd=hd, ch=ch, w2=w2: e.match_replace(out=w2, in_to_replace=best_s[:, hd, 0:8],
                                                                      in_values=ch, imm_value=NEG),
                 reads=[CAND, BB], writes=[W2])
            P.op("dve", lambda e, hd=hd, ch=ch: e.max_index(out=best_pos[:, hd, 0:8], in_max=best_s[:, hd, 0:8], in_values=ch),
                 reads=[CAND, BB], writes=[BB])
            P.op("dve", lambda e, hd=hd, w2=w2: e.max(out=best_s[:, hd, 8:16], in_=w2), reads=[W2, BB], writes=[BB])
            P.op("dve", lambda e, hd=hd, w2=w2: e.max_index(out=best_pos[:, hd, 8:16], in_max=best_s[:, hd, 8:16], in_values=w2),
                 reads=[W2, BB], writes=[BB])
        bp = best_pos[:].rearrange("p a b -> p (a b)")
        P.op("dve", lambda e: e.tensor_single_scalar(out=iju[:, 0, :], in_=bp, scalar=4, op=ALU.logical_shift_right),
             reads=BEST, writes=[IJU])
        P.op("dve", lambda e: e.tensor_single_scalar(out=iju[:, 1, :], in_=bp, scalar=15, op=ALU.bitwise_and),
             reads=BEST + [IJU], writes=[IJU])
        P.op("dve", lambda e: e.tensor_copy(out=ijf[:], in_=iju[:]), reads=[IJU], writes=[IJF])
        oh3 = oh.rearrange("p (a b) -> p a b", b=16)
        oh4 = oh.rearrange("p (a b c) -> p a b c", a=8, b=16)
        pr4 = prod.rearrange("p (a b c) -> p a b c", a=8, b=16)
        pr3 = prod.rearrange("p (a b) -> p a b", b=16)
        for w in range(2):
            P.op("dve", lambda e, w=w: e.tensor_tensor(
                out=oh3, in0=ijf[:, w, :].unsqueeze(2).broadcast_to([128, 128, 16]),
                in1=iota16[:].unsqueeze(1).broadcast_to([128, 128, 16]), op=ALU.is_equal),
                reads=[IJF, CONSTS], writes=[OH])
            P.op("dve", lambda e, w=w: e.tensor_tensor(
                out=pr4, in0=oh4, in1=top_if[:, :, w, :].unsqueeze(2).broadcast_to([128, 8, 16, 16]), op=ALU.mult),
                reads=[OH, TOPIF], writes=[PROD])
            P.op("dve", lambda e, w=w: e.tensor_reduce(out=i12[:, w, :], in_=pr3, axis=AX.X, op=ALU.add),
                 reads=[PROD], writes=[I12])
        P.op("dve", lambda e: e.scalar_tensor_tensor(out=idxf[:], in0=i12[:, 0, :], scalar=128.0, in1=i12[:, 1, :],
                                                     op0=ALU.mult, op1=ALU.add), reads=[I12], writes=[IDXF])
        P.op("dve", lambda e: e.tensor_copy(out=idx32[:], in_=idxf[:]), reads=[IDXF], writes=[IDX])
        P.op("dve", lambda e: e.tensor_tensor(out=ev_[:], in0=best_s[:], in1=best_s[:, :, 0:1].broadcast_to([128, 8, 16]),
                                              op=ALU.subtract), reads=BEST, writes=[ESM])
        P.op("act", act_fn(ev_[:], ev_[:], AF.Exp), reads=[ESM], writes=[ESM])
        P.op("dve", lambda e: e.tensor_reduce(out=ssum[:, 0:8], in_=ev_[:], axis=AX.X, op=ALU.add), reads=[ESM], writes=[SSUM])
        P.op("dve", lambda e: e.reciprocal(out=ssum[:, 0:8], in_=ssum[:, 0:8]), reads=[SSUM], writes=[SSUM])
        P.op("dve", lambda e: e.tensor_tensor(out=gsm[:], in0=ev_[:], in1=ssum[:, 0:8].unsqueeze(2).broadcast_to([128, 8, 16]),
                                              op=ALU.mult), reads=[ESM, SSUM], writes=[GSM])
        if t == dbg_tile:
            dump("h2b", h2b[:], [128, D], BF16, H2B)
            dump("best_s", best_s[:], [128, 8, 16], F32, BEST[7])
            dump("idx32", idx32[:], [128, 128], I32, IDX)
            dump("gsm", gsm[:], [128, 8, 16], F32, GSM)


    def post(t, recs):
        x1 = x1s[t % 2]; X1 = X1s[t % 2]
        h2b = h2bs[t % 2]; H2B = H2Bs[t % 2]
        idx32 = idx32s[t % 2]; IDX = IDXs[t % 2]
        gsm = gsms[t % 2]; GSM = GSMs[t % 2]
        gflat = gsm[:].rearrange("p a b -> p (a b)")
        ri = 0
        slots_of = schedule(recs, gap=GAP)
        order = sorted(range(len(recs)), key=lambda i: (slots_of[i], i))
        if t == 0 and recs:
            print("pre-phase schedule span", max(slots_of), "records", len(recs))
        def tail(s):
            SB_ = SLOTB[s % NSL]
            jv = jvs[s]
            P.op("act", act_fn(coefv[:, s:s + 1], gev[:, s:s + 1], AF.Copy, scale=gflat[:, s:s + 1]),
                 reads=[SB_, GSM], writes=[SB_])
            dj = s % 4
            P.op("act", act_fn(diag[:, dj, :], ident_bf[:], AF.Copy, scale=coefv[:, s:s + 1]),
                 reads=[SB_, CONSTS], writes=[DIAG[dj]])

            def vmm(e, jv=jv, s=s, dj=dj):
                for half in range(2):
                    r_ = e.matmul(ps[:, 5 + half, :], lhsT=diag[:, dj, :],
                                  rhs=rv(jv)[:, half * 512:(half + 1) * 512],
                                  start=(s == 0), stop=(s == 127))
                return r_
            P.op("pe", vmm, reads=[DIAG[dj], RV[jv]], writes=[PB[5], PB[6]], skip_self=(s > 0))

        jvs = {}
        for s in range(128):
            g_ = gather_ctr[0]
            gather_ctr[0] += 1
            ju, jv = g_ % NU, g_ % NVR
            jvs[s] = jv
            extra = uv_tokens if first_gather[0] else ()
            first_gather[0] = False
            P.dma("pool", lambda e, ju=ju, jv=jv, s=s: e.indirect_dma_start(
                out=ruv(ju, jv), out_offset=None, in_=uv_d[:, :],
                in_offset=bass.IndirectOffsetOnAxis(ap=idx32[:, s:s + 1], axis=0)),
                f"gth{ju}", reads=[IDX], writes=[RU[ju], RV[jv]], extra=extra)
            SB_ = SLOTB[s % NSL]
            P.op("dve", lambda e, ju=ju, s=s: e.scalar_tensor_tensor(
                out=junkD[:], in0=ru(ju), scalar=1.0, in1=h2b[:], op0=ALU.mult, op1=ALU.mult,
                accum_out=actv[:, s:s + 1]), reads=[RU[ju], H2B], writes=[JUNKD, SB_])
            P.op("act", act_fn(gev[:, s:s + 1], actv[:, s:s + 1], AF.Gelu), reads=[SB_], writes=[SB_])
            if s >= SKEW:
                tail(s - SKEW)
            while ri < len(order) and slots_of[order[ri]] <= s:
                P.commit(recs[order[ri]])
                ri += 1
        for s in range(128 - SKEW, 128):
            tail(s)
        if t == dbg_tile:
            dump("actv", actv[:], [128, 128], F32, SLOTB[3])

        while ri < len(order):
            P.commit(recs[order[ri]])
            ri += 1
        P.op("dve", lambda e: e.tensor_tensor(out=ot[:].rearrange("p (a b) -> p a b", a=2), in0=ps[:, 5:7, :],
                                              in1=g2_bc[:].rearrange("p (a b) -> p a b", a=2), op=ALU.mult),
             reads=[PB[5], PB[6], CONSTS], writes=[OT])
        P.op("dve", lambda e: e.tensor_tensor(out=ot[:], in0=ot[:], in1=x1[:], op=ALU.add), reads=[OT, X1], writes=[OT])
        r3 = rms_stats(ot[:], OT, 2, h[:], H)
        P.op("dve", lambda e: e.scalar_tensor_tensor(out=ot[:], in0=ot[:], scalar=r3, in1=fg_bc[:],
                                                     op0=ALU.mult, op1=ALU.mult),
             reads=[OT, STAT[2], CONSTS], writes=[OT])
        return P.dma("sp", lambda e: e.dma_start(out=out_d[t * 128:(t + 1) * 128, :], in_=ot[:]), "st_out", reads=[OT])

    pre(0)
    last = None
    for t in range(NT):
        recs = []
        if t + 1 < NT:
            P.defer = []
            pre(t + 1)
            recs = P.defer
            P.defer = None
        last = post(t, recs)
    fin = [last] + [(k, v) for k, v in P.dsem.items() if k.startswith("dbg_")]
    P.wait_only("sp", fin)

    sems = {}
    for e in ENG:
        sems[e] = es.enter_context(nc.semaphore("pc_" + e))
    for k in P.dsem:
        sems[k] = es.enter_context(nc.semaphore("d_" + k))
    blockname = {"pe": "tensor", "act": "scalar", "dve": "vector", "pool": "gpsimd", "sp": "sync"}
    with nc.Block() as block:
        for e in ENG:
            def body(eng, e=e):
                for waits, fn, inc in P.ops[e]:
                    for k, v in waits:
                        eng.wait_ge(sems[k], v)
                    if fn is None:
                        continue
                    inst = fn(eng)
                    inst.then_inc(sems[inc[0]], inc[1])
            getattr(block, blockname[e])(body)
    es.close()
    return nc, list(dbg_out.keys())


def _kc_layout(w):
    n = w.shape[1]
    return np.ascontiguousarray(w.reshape(8, 128, n).transpose(1, 0, 2))


def _col4(v):
    return np.ascontiguousarray(v.reshape(4, 128).T)


def _blockdiag(w):
    o = np.zeros((128, 4, 128), np.float32)
    for hd in range(8):
        c, q = hd // 2, hd % 2
        o[q * 64:(q + 1) * 64, c, q * 64:(q + 1) * 64] = w[hd]
    return o


def prep_shared(inp):
    f = lambda a: np.asarray(a, dtype=np.float32)
    vecs = np.zeros((128, NV), np.float32)
    caw = f(inp["conv_a_w"])[0]
    cbw = f(inp["conv_b_w"])[0]
    for k in range(3):
        vecs[:, V_WA + 4 * k:V_WA + 4 * k + 4] = _col4(caw[k])
    for k in range(4):
        vecs[:, V_WB + 4 * k:V_WB + 4 * k + 4] = _col4(cbw[k])
    vecs[:, V_CBB:V_CBB + 4] = _col4(f(inp["conv_b_b"])[0])
    vecs[:, V_BR:V_BR + 4] = _col4(f(inp["b_r"])[0].reshape(512))
    vecs[:, V_BI:V_BI + 4] = _col4(f(inp["b_i"])[0].reshape(512))
    vecs[:, V_LAM:V_LAM + 4] = _col4(f(inp["lru_lambda"])[0].reshape(512))
    vecs[:, V_GNA:V_GNA + 4] = _col4(f(inp["gn_a"])[0])
    vecs[:, V_GNB:V_GNB + 4] = _col4(f(inp["gn_b"])[0])
    sk = f(inp["sub_keys"])[0]
    skT1 = sk.transpose(1, 3, 0, 2).reshape(128, 8, 128)
    skT = np.zeros((128, 2, 8, 128), np.float32)
    skT[0:64, 0] = skT1[0:64]
    skT[64:128, 1] = skT1[64:128]
    grow = np.concatenate([f(inp["norm1_g"])[0], f(inp["norm2_g"])[0], f(inp["final_g"])])[None, :]
    return {
        "w_ada": _kc_layout(f(inp["w_ada"])[0]),
        "b_ada": np.ascontiguousarray(f(inp["b_ada"])[0][None, :]),
        "grow": np.ascontiguousarray(grow),
        "w_in": _kc_layout(f(inp["w_in"])[0]),
        "w_out": _kc_layout(f(inp["w_out"])[0]),
        "w_q": _kc_layout(f(inp["w_q"])[0]),
        "vecs": vecs,
        "wr_bd": _blockdiag(f(inp["w_r"])[0]),
        "wi_bd": _blockdiag(f(inp["w_i"])[0]),
        "skT": skT,
        "expert_u": np.ascontiguousarray(f(inp["expert_u"])[0]),
        "expert_v": np.ascontiguousarray(f(inp["expert_v"])[0]),
    }


def kernel(**inputs):
    x = np.asarray(inputs["x"], dtype=np.float32)
    c = np.asarray(inputs["c"], dtype=np.float32)
    B, S, _ = x.shape
    shared = prep_shared(inputs)
    nc, _ = build(S // 128)
    in_maps = []
    for b in range(B):
        m = dict(shared)
        m["x"] = np.ascontiguousarray(x[b])
        m["cin"] = np.ascontiguousarray(c[b].reshape(8, 128).T)
        in_maps.append(m)
    res = run_bass_kernel_spmd(nc, in_maps, core_ids=list(range(B)))
    return np.stack([np.asarray(r["out"]) for r in res.results], axis=0).astype(np.float32)
```

```python
import contextlib
import numpy as np
import concourse.bass as bass
import concourse.mybir as mybir
from concourse.bass_utils import run_bass_kernel_spmd

F32 = mybir.dt.float32
BF16 = mybir.dt.bfloat16
I32 = mybir.dt.int32
U32 = mybir.dt.uint32
ALU = mybir.AluOpType
AF = mybir.ActivationFunctionType
AX = mybir.AxisListType

D = 1024
NIN = 2560
NE = 16384
EPS = 1e-6
NU = 7
NVR = 7
NSL = 16
SKEW = 0
NEG = -1.0e30
GAP_DVE = 3
GAP = 1
UV_KIND = "Internal"
ENG = ["pe", "act", "dve", "pool", "sp"]

V_WA = 0
V_WB = 12
V_CBB = 28
V_BR = 32
V_BI = 36
V_LAM = 40
V_GNA = 44
V_GNB = 48
NV = 52


class Buf:
    def __init__(self, name, const=False):
        self.name = name
        self.w = None
        self.r = []
        self.const = const


class Plan:
    def __init__(self):
        self.ops = {e: [] for e in ENG}
        self.cnt = {e: 0 for e in ENG}
        self.waited = {e: {} for e in ENG}
        self.dsem = {}
        self.defer = None

    def handoff(self, srcs, dsts):
        if self.defer is not None:
            self.defer.append(("handoff", list(srcs), list(dsts)))
            return
        handoff(srcs, dsts)

    def commit(self, rec):
        assert self.defer is None
        if rec[0] == "op":
            self.op(*rec[1:])
        elif rec[0] == "dma":
            self.dma(*rec[1:])
        else:
            handoff(rec[1], rec[2])

    def _deps(self, e, reads, writes, extra, skip_self):
        toks = list(extra)
        for b in reads:
            toks.append(b.w)
        for b in writes:
            toks.append(b.w)
            toks.extend(b.r)
        best = {}
        for t in toks:
            if t is None:
                continue
            k, v = t
            if skip_self and k == e:
                continue
            if best.get(k, 0) < v:
                best[k] = v
        waits = []
        for k, v in best.items():
            if self.waited[e].get(k, 0) >= v:
                continue
            self.waited[e][k] = v
            waits.append((k, v))
        return waits

    def _upd(self, tok, reads, writes):
        for b in reads:
            if not b.const:
                b.r.append(tok)
        for b in writes:
            b.w = tok
            b.r = []

    def op(self, e, fn, reads=(), writes=(), extra=(), skip_self=False):
        if self.defer is not None:
            self.defer.append(("op", e, fn, tuple(reads), tuple(writes), tuple(extra), skip_self))
            return None
        waits = self._deps(e, reads, writes, extra, skip_self)
        self.cnt[e] += 1
        tok = (e, self.cnt[e])
        self.ops[e].append((waits, fn, (e, 1)))
        self._upd(tok, reads, writes)
        return tok

    def dma(self, q, fn, sem, reads=(), writes=(), extra=()):
        if self.defer is not None:
            self.defer.append(("dma", q, fn, sem, tuple(reads), tuple(writes), tuple(extra)))
            return None
        waits = self._deps(q, reads, writes, extra, False)
        self.dsem[sem] = self.dsem.get(sem, 0) + 16
        tok = (sem, self.dsem[sem])
        self.ops[q].append((waits, fn, (sem, 16)))
        self._upd(tok, reads, writes)
        return tok

    def wait_only(self, e, toks):
        waits = self._deps(e, (), (), toks, False)
        if waits:
            self.ops[e].append((waits, None, None))


def schedule(recs, gap=2, gap_dma=4, cap=None):
    cap = cap or {"dve": 2, "act": 3, "pe": 3, "pool": 2, "sp": 4}
    lastw, readers = {}, {}
    slot, eng = [], []
    load = {}
    for i, rec in enumerate(recs):
        if rec[0] == "op":
            e, reads, writes = rec[1], rec[3], rec[4]
        elif rec[0] == "dma":
            e, reads, writes = rec[1], rec[4], rec[5]
        else:
            e, reads, writes = None, rec[1], rec[2]
        deps = set()
        for b in reads:
            if id(b) in lastw:
                deps.add(lastw[id(b)])
        for b in writes:
            if id(b) in lastw:
                deps.add(lastw[id(b)])
            deps.update(readers.get(id(b), ()))
        s0 = 0
        for d in deps:
            if eng[d] == e or eng[d] is None or e is None:
                g = 0
            else:
                g = gap_dma if recs[d][0] == "dma" else ((3 if eng[d] == "act" else GAP_DVE) if e == "dve" else gap)
            s0 = max(s0, slot[d] + g)
        if e is not None:
            while load.get((e, s0), 0) >= cap[e]:
                s0 += 1
            load[(e, s0)] = load.get((e, s0), 0) + 1
        slot.append(s0)
        eng.append(e)
        for b in reads:
            readers.setdefault(id(b), []).append(i)
        for b in writes:
            lastw[id(b)] = i
            readers[id(b)] = []
    return slot


def handoff(srcs, dsts):
    toks = []
    for b in srcs:
        if b.w is not None:
            toks.append(b.w)
        toks.extend(b.r)
    for d in dsts:
        d.r.extend(toks)


def build(NT, dbg_tile=None, stage=99):
    S = NT * 128
    nc = bass.Bass("TRN2", target_bir_lowering=False)
    P = Plan()
    es = contextlib.ExitStack()

    def din(name, shape, dt=F32):
        return nc.dram_tensor(name, shape, dt, kind="ExternalInput").ap()

    x_d = din("x", [S, D])
    cin_d = din("cin", [128, 8])
    wada_d = din("w_ada", [128, 8, 6 * D])
    bada_d = din("b_ada", [1, 6 * D])
    grow_d = din("grow", [1, 3 * D])
    win_d = din("w_in", [128, 8, NIN])
    wout_d = din("w_out", [128, 8, D])
    wq_d = din("w_q", [128, 8, D])
    vecs_d = din("vecs", [128, NV])
    wrbd_d = din("wr_bd", [128, 4, 128])
    wibd_d = din("wi_bd", [128, 4, 128])
    skT_d = din("skT", [128, 2, 8, 128])
    eu_d = din("expert_u", [NE, D])
    ev_d = din("expert_v", [NE, D])
    out_d = nc.dram_tensor("out", [S, D], F32, kind="ExternalOutput").ap()
    uv_d = nc.dram_tensor("uvtab", [NE, 2 * D], BF16, kind=UV_KIND).ap()

    dbg_out = {}

    def sb(name, shape, dt):
        return es.enter_context(nc.sbuf_tensor(name, shape, dt))

    w_in_bf = sb("w_in_bf", [128, 8, NIN], BF16)
    w_out_bf = sb("w_out_bf", [128, 8, D], BF16)
    w_q_bf = sb("w_q_bf", [128, 8, D], BF16)
    skT = sb("skT_sb", [128, 2, 8, 128], F32)
    wr_bf = sb("wr_bf", [128, 4, 128], BF16)
    wi_bf = sb("wi_bf", [128, 4, 128], BF16)
    identf = sb("identf", [128, 128], F32)
    ident_bf = sb("ident_bf", [128, 128], BF16)
    blockones = sb("blockones", [128, 128], F32)
    ones_row = sb("ones_row", [1, 128], F32)
    vecs = sb("vecs_sb", [128, NV], F32)
    nsp = sb("nsp", [128, 8], F32)
    iota16 = sb("iota16", [128, 16], F32)
    gmod1_bc = sb("gmod1_bc", [128, D], F32)
    sh1_bc = sb("sh1_bc", [128, D], F32)
    g1_bc = sb("g1_bc", [128, D], F32)
    gmod2_bc = sb("gmod2_bc", [128, D], F32)
    sh2_bc = sb("sh2_bc", [128, D], F32)
    g2_bc = sb("g2_bc", [128, D], F32)
    fg_bc = sb("fg_bc", [128, D], F32)
    CONSTS = Buf("consts")

    xt = [sb("xt0", [128, D], F32)] * 2
    XT = [Buf("xt0")] * 2
    x1s = [sb(f"x1_{i}", [128, D], F32) for i in range(2)]; X1s = [Buf(f"x1_{i}") for i in range(2)]
    ot = sb("ot", [128, D], F32); OT = Buf("ot")
    h = sb("h", [128, D], BF16); H = Buf("h")
    hT = sb("hT", [128, 8, 128], BF16); HT = Buf("hT")
    h2bs = [sb(f"h2b{i}", [128, D], BF16) for i in range(2)]; H2Bs = [Buf(f"h2b{i}") for i in range(2)]
    h2T = sb("h2T", [128, 8, 128], BF16); H2T = Buf("h2T")
    junkD = sb("junkD", [128, D], BF16); JUNKD = Buf("junkD")
    stat = sb("stat", [128, 16], F32)
    STAT = [Buf(f"stat{i}") for i in range(3)]
    mbuf = sb("mbuf", [128, 4, 130], F32); MBUF = Buf("mbuf")
    xrbuf = sb("xrbuf", [128, 4, 131], F32); XRBUF = Buf("xrbuf")
    xcb = sb("xcb", [128, 4, 128], BF16); XCB = Buf("xcb")
    ynT = sb("ynT", [128, 8, 128], BF16)
    YNT = [Buf("ynTa"), Buf("ynTb")]
    hstate = sb("hstate", [128, 4], F32); HST = Buf("hstate")

    uni = sb("uni", [128, 5120], F32)

    def uview(off, n, shape3=None):
        v = uni[:, off:off + n]
        return v

    gc_sb = uni[:, 0:512]; GC = Buf("gc")
    ca = uni[:, 512:1024]; CA = [Buf(f"ca{c}") for c in range(4)]
    sq = uni[:, 1024:1536]; SQ = Buf("sq")
    rstd = uni[:, 1536:2048]; RSTD = Buf("rstd")
    xc = uni[:, 2048:2560]; XC = [Buf(f"xc{c}") for c in range(4)]
    rr = uni[:, 2560:3072]; RR = Buf("rr")
    ii_ = uni[:, 3072:3584]; II = Buf("ii")
    aa = uni[:, 3584:4096]; AA = Buf("aa")
    hs = uni[:, 4096:4608]; HS = Buf("hs")
    gg = uni[:, 4608:5120]; GG = Buf("gg")
    MIXB = [GC, SQ, RSTD, RR, II, AA, HS, GG] + CA + XC
    cand = uni[:, 0:2048]; CAND = Buf("cand")
    oh = uni[:, 2048:3072].bitcast(BF16); OH = Buf("oh")
    prod = uni[:, 3072:4096].bitcast(BF16); PROD = Buf("prod")
    qT = uni[:, 4096:5120]; QT = Buf("qT")
    PEERB = [CAND, OH, PROD, QT]

    top_s = sb("top_s", [128, 8, 2, 16], F32)
    top_i = sb("top_i", [128, 8, 2, 16], U32)
    TOP = [[Buf(f"top{hd}_{c}") for c in range(2)] for hd in range(8)]
    top_if = sb("top_if", [128, 8, 2, 16], F32); TOPIF = Buf("top_if")
    work = uni[:, 3072:3328].rearrange("p (a b) -> p a b", a=2); WORK = [PROD, PROD]
    work2 = uni[:, 2048:2560].rearrange("p (a b) -> p a b", a=2); WORK2 = [OH, OH]
    best_s = sb("best_s", [128, 8, 16], F32)
    best_pos = sb("best_pos", [128, 8, 16], U32)
    BEST = [Buf(f"best{hd}") for hd in range(8)]
    iju = sb("iju", [128, 2, 128], U32); IJU = Buf("iju")
    ijf = sb("ijf", [128, 2, 128], F32); IJF = Buf("ijf")
    i12 = sb("i12", [128, 2, 128], F32); I12 = Buf("i12")
    idxf = sb("idxf", [128, 128], F32); IDXF = Buf("idxf")
    idx32s = [sb(f"idx32_{i}", [128, 128], I32) for i in range(2)]; IDXs = [Buf(f"idx32_{i}") for i in range(2)]
    ev_ = sb("esm", [128, 8, 16], F32); ESM = Buf("esm")
    gsms = [sb(f"gsm{i}", [128, 8, 16], F32) for i in range(2)]; GSMs = [Buf(f"gsm{i}") for i in range(2)]
    ssum = sb("ssum", [128, 16], F32); SSUM = Buf("ssum")
    actv = sb("actv", [128, 128], F32)
    gev = sb("gev", [128, 128], F32)
    coefv = sb("coefv", [128, 128], F32)
    SLOTB = [Buf(f"slotb{i}") for i in range(NSL)]
    diag = sb("diag", [128, 4, 128], BF16)
    DIAG = [Buf(f"diag{i}") for i in range(4)]
    ringt = sb("ring", [128, (NU + NVR) * D], BF16)
    RING = [Buf(f"stg{j}") for j in range(6)]
    RU = [Buf(f"ru{j}") for j in range(NU)]
    RV = [Buf(f"rv{j}") for j in range(NVR)]
    ring = ringt[:, 0:6 * 2 * D].rearrange("p (a b) -> p a b", a=6)
    ringf = ringt[:].bitcast(F32)

    def ru(j):
        return ringt[:, 2 * j * D:(2 * j + 1) * D]

    def rv(j):
        return ringt[:, (2 * j + 1) * D:(2 * j + 2) * D]

    def ruv(ju, jv):
        assert ju == jv
        return ringt[:, 2 * ju * D:(2 * ju + 2) * D]
    small = uni[0:1, 0:1536].rearrange("p (a b) -> p a b", a=3); SMALL = Buf("small")

    ps = es.enter_context(nc.psum_tensor("ps", [128, 8, 512], F32))
    PB = [Buf(f"psb{b}") for b in range(8)]
    trb = ps[:, 0, :].bitcast(BF16).rearrange("p (c t) -> p c t", c=8)

    WIN = Buf("w_in_bf"); WIN2 = Buf("w_in_bf2"); win_tokens = []; WOUT = Buf("w_out_bf"); WQ = Buf("w_q_bf")

    def act_fn(out, in_, func, bias=None, scale=None, accum=None):
        kw = {}
        if bias is not None:
            kw["bias"] = bias
        if scale is not None:
            kw["scale"] = scale
        if accum is not None:
            kw["accum_out"] = accum
        return lambda e: e.activation(out, in_, func, **kw)

    def dump(name, ap, shape, dt, buf):
        if dbg_tile is None:
            return
        dd = nc.dram_tensor("dbg_" + name, list(shape), dt, kind="ExternalOutput").ap()
        dbg_out[name] = True
        P.dma("sp", lambda e: e.dma_start(out=dd, in_=ap), "dbg_" + name, reads=[buf])

    RALL = RING

    onesf = ringf[:, 0:128]
    P.op("pool", lambda e: e.memset(onesf, 1.0), writes=RALL)
    P.op("pool", lambda e: e.affine_select(out=identf[:], in_=onesf, pattern=[[-1, 128]],
                                           compare_op=ALU.is_equal, fill=0.0, base=0,
                                           channel_multiplier=1),
         reads=RALL, writes=[CONSTS])
    P.op("pool", lambda e: e.tensor_copy(out=ident_bf[:], in_=identf[:]), reads=[CONSTS], writes=[CONSTS])

    P.op("pool", lambda e: e.memset(blockones[:], 0.0), writes=[CONSTS])
    P.op("pool", lambda e: e.memset(blockones[0:64, 0:64], 1.0 / 64), writes=[CONSTS])
    P.op("pool", lambda e: e.memset(blockones[64:128, 64:128], 1.0 / 64), writes=[CONSTS])
    P.op("pool", lambda e: e.memset(ones_row[:], 1.0), writes=[CONSTS])
    P.op("pool", lambda e: e.iota(iota16[:], pattern=[[1, 16]], base=0, channel_multiplier=0,
                                  allow_small_or_imprecise_dtypes=True), writes=[CONSTS])
    P.op("pool", lambda e: e.memset(mbuf[:], 0.0), writes=[MBUF])
    P.op("pool", lambda e: e.memset(xrbuf[:], 0.0), writes=[XRBUF])
    P.op("pool", lambda e: e.memset(hstate[:], 0.0), writes=[HST])

    VEC = Buf("vecs")
    P.dma("sp", lambda e: e.dma_start(out=vecs[:], in_=vecs_d[:, :]), "ld_vecs", writes=[VEC])
    SKT = Buf("skT")
    P.dma("sp", lambda e: e.dma_start(out=skT[:], in_=skT_d[:, :, :, :]), "ld_skT", writes=[SKT])
    cs = stat[:, 8:16]
    CS = Buf("cs")
    P.dma("sp", lambda e: e.dma_start(out=cs, in_=cin_d[:, :]), "ld_cs", writes=[CS])
    P.op("act", act_fn(cs, cs, AF.Silu), reads=[CS], writes=[CS])

    stg = ringf
    P.dma("sp", lambda e: e.dma_start(out=stg[:, 0:512], in_=wrbd_d[:, :, :].rearrange("p a b -> p (a b)")),
          "ld_stg", writes=RALL)
    P.op("dve", lambda e: e.tensor_copy(out=wr_bf[:].rearrange("p a b -> p (a b)"), in_=stg[:, 0:512]),
         reads=RALL, writes=[CONSTS])
    P.dma("sp", lambda e: e.dma_start(out=stg[:, 0:512], in_=wibd_d[:, :, :].rearrange("p a b -> p (a b)")),
          "ld_stg", writes=RALL)
    P.op("dve", lambda e: e.tensor_copy(out=wi_bf[:].rearrange("p a b -> p (a b)"), in_=stg[:, 0:512]),
         reads=RALL, writes=[CONSTS])

    def nsp_ops():
        lam = vecs[:, V_LAM:V_LAM + 4]
        e_ = stat[:, 0:4]
        t1 = stat[:, 4:8]
        NS = Buf("nsp_tmp")
        P.op("act", act_fn(e_, lam, AF.Exp, scale=-1.0), reads=[VEC], writes=[NS])
        P.op("dve", lambda e: e.tensor_scalar(out=t1, in0=e_, scalar1=-0.2, scalar2=0.25, op0=ALU.mult, op1=ALU.add),
             reads=[NS], writes=[NS])
        for cst in (1.0 / 3, 0.5, 1.0):
            P.op("dve", lambda e: e.tensor_tensor(out=t1, in0=t1, in1=e_, op=ALU.mult), reads=[NS], writes=[NS])
            P.op("dve", lambda e, cst=cst: e.tensor_scalar(out=t1, in0=t1, scalar1=-1.0, scalar2=cst,
                                                          op0=ALU.mult, op1=ALU.add), reads=[NS], writes=[NS])
        P.op("dve", lambda e: e.tensor_tensor(out=t1, in0=t1, in1=e_, op=ALU.mult), reads=[NS], writes=[NS])
        l1 = nsp[:, 4:8]
        P.op("act", act_fn(l1, e_, AF.Ln, bias=1.0), reads=[NS], writes=[NS])
        msk = nsp[:, 0:4]
        P.op("dve", lambda e: e.tensor_single_scalar(out=msk, in_=e_, scalar=0.05, op=ALU.is_lt), reads=[NS], writes=[NS])
        P.op("dve", lambda e: e.tensor_tensor(out=t1, in0=t1, in1=l1, op=ALU.subtract), reads=[NS], writes=[NS])
        P.op("dve", lambda e: e.tensor_tensor(out=t1, in0=t1, in1=msk, op=ALU.mult), reads=[NS], writes=[NS])
        P.op("dve", lambda e: e.tensor_tensor(out=t1, in0=t1, in1=l1, op=ALU.add), reads=[NS], writes=[NS])
        P.op("dve", lambda e: e.tensor_scalar(out=nsp[:, 0:4], in0=t1, scalar1=-8.0, scalar2=None, op0=ALU.mult),
             reads=[NS], writes=[NS])
        P.op("dve", lambda e: e.tensor_scalar(out=nsp[:, 4:8], in0=t1, scalar1=-16.0, scalar2=None, op0=ALU.mult),
             reads=[NS], writes=[CONSTS, NS])
    nsp_ops()

    bc_dst = [sh1_bc, gmod1_bc, g1_bc, sh2_bc, gmod2_bc, g2_bc]
    wst = stg[:, 0:4096].rearrange("p (k n) -> p k n", k=8)
    for j in range(12):
        v, half = j // 2, j % 2
        P.dma("sp", lambda e, j=j: e.dma_start(out=wst, in_=wada_d[:, :, j * 512:(j + 1) * 512]),
              "ld_stg", writes=RALL)
        P.dma("sp", lambda e, j=j: e.dma_start(out=small[0:1, 0, :], in_=bada_d[0:1, j * 512:(j + 1) * 512]),
              "ld_small", writes=[SMALL])
        if v in (1, 4):
            go = (0 if v == 1 else 1) * D + half * 512
            P.dma("sp", lambda e, go=go: e.dma_start(out=small[0:1, 1, :], in_=grow_d[0:1, go:go + 512]),
                  "ld_small", writes=[SMALL])

        def mm_ada(e):
            for kc in range(8):
                r_ = e.matmul(ps[0:1, 6, :], lhsT=cs[:, kc:kc + 1], rhs=wst[:, kc, :], start=(kc == 0), stop=(kc == 7))
            return r_
        P.op("pe", mm_ada, reads=RALL + [CS], writes=[PB[6]])
        row = small[0:1, 2, :]
        P.op("dve", lambda e: e.tensor_tensor(out=row, in0=ps[0:1, 6, :], in1=small[0:1, 0, :], op=ALU.add),
             reads=[PB[6], SMALL], writes=[SMALL])
        if v in (1, 4):
            P.op("dve", lambda e: e.scalar_tensor_tensor(out=row, in0=row, scalar=1.0, in1=small[0:1, 1, :],
                                                         op0=ALU.add, op1=ALU.mult),
                 reads=[SMALL], writes=[SMALL])
        P.op("pe", lambda e: e.matmul(ps[:, 7, :], lhsT=ones_row[0:1, :], rhs=row, start=True, stop=True),
             reads=[SMALL, CONSTS], writes=[PB[7]])
        dst = bc_dst[v]
        P.op("act", lambda e, dst=dst, half=half: e.copy(out=dst[:, half * 512:(half + 1) * 512], in_=ps[:, 7, :]),
             reads=[PB[7]], writes=[CONSTS])
    for half in range(2):
        go = 2 * D + half * 512
        P.dma("sp", lambda e, go=go: e.dma_start(out=small[0:1, 1, :], in_=grow_d[0:1, go:go + 512]),
              "ld_small", writes=[SMALL])
        P.op("pe", lambda e: e.matmul(ps[:, 7, :], lhsT=ones_row[0:1, :], rhs=small[0:1, 1, :], start=True, stop=True),
             reads=[SMALL, CONSTS], writes=[PB[7]])
        P.op("act", lambda e, half=half: e.copy(out=fg_bc[:, half * 512:(half + 1) * 512], in_=ps[:, 7, :]),
             reads=[PB[7]], writes=[CONSTS])

    for pc in range(4):
        v5 = stg[:, 0:5120].rearrange("p (k n) -> p k n", k=2)
        P.dma("sp", lambda e, pc=pc: e.dma_start(out=v5, in_=win_d[:, 2 * pc:2 * pc + 2, :]), "ld_stg", writes=RALL)
        P.op("act", lambda e, pc=pc: e.copy(out=w_in_bf[:, 2 * pc, :], in_=v5[:, 0, :]), reads=RALL, writes=[WIN])
        P.op("dve", lambda e, pc=pc: e.tensor_copy(out=w_in_bf[:, 2 * pc + 1, :], in_=v5[:, 1, :]), reads=RALL, writes=[WIN])
    for (src_d, dst, DB) in ((wout_d, w_out_bf, WOUT), (wq_d, w_q_bf, WQ)):
        for pc in range(2):
            v4 = stg[:, 0:4096].rearrange("p (k n) -> p k n", k=4)
            P.dma("sp", lambda e, pc=pc, src_d=src_d: e.dma_start(out=v4, in_=src_d[:, 4 * pc:4 * pc + 4, :]),
                  "ld_stg", writes=RALL)
            P.op("act", lambda e, pc=pc, dst=dst: e.copy(out=dst[:, 4 * pc:4 * pc + 2, :], in_=v4[:, 0:2, :]),
                 reads=RALL, writes=[DB])
            P.op("dve", lambda e, pc=pc, dst=dst: e.tensor_copy(out=dst[:, 4 * pc + 2:4 * pc + 4, :], in_=v4[:, 2:4, :]),
                 reads=RALL, writes=[DB])

    uv_tokens = []
    NBLK = NE // 128

    def uv_loads(r):
        st_ = r % 2
        sU, sV = RING[3 * st_], RING[3 * st_ + 1]
        fu = ringf[:, (3 * st_) * 1024:(3 * st_ + 1) * 1024]
        fv = ringf[:, (3 * st_ + 1) * 1024:(3 * st_ + 2) * 1024]
        P.dma("sp", lambda e: e.dma_start(out=fu, in_=eu_d[r * 128:(r + 1) * 128, :]), f"stg{3 * st_}", writes=[sU])
        P.dma("sp", lambda e: e.dma_start(out=fv, in_=ev_d[r * 128:(r + 1) * 128, :]), f"stg{3 * st_ + 1}", writes=[sV])

    if stage >= 1:
        uv_loads(0)
    for r in range(NBLK if stage >= 1 else 0):
        if r + 1 < NBLK:
            uv_loads(r + 1)
        st_ = r % 2
        sU, sV, sO = RING[3 * st_], RING[3 * st_ + 1], RING[3 * st_ + 2]
        fu = ringf[:, (3 * st_) * 1024:(3 * st_ + 1) * 1024]
        fv = ringf[:, (3 * st_ + 1) * 1024:(3 * st_ + 2) * 1024]
        o = ring[:, 3 * st_ + 2, :]
        OU = Buf("tmp")
        t1 = P.op("act", lambda e, o=o, fu=fu: e.copy(out=o[:, 0:D], in_=fu), reads=[sU], writes=[sO])
        t2 = P.op("dve", lambda e, o=o, fv=fv: e.tensor_copy(out=o[:, D:2 * D], in_=fv), reads=[sV, sO], writes=[OU])
        tk = P.dma("sp", lambda e, o=o, r=r: e.dma_start(out=uv_d[r * 128:(r + 1) * 128, :], in_=o),
                   f"stg{3 * st_ + 2}", reads=[sO, OU])
        uv_tokens.append(tk)

    for b in (WIN, WOUT, WQ, CONSTS, VEC, SKT):
        b.const = True
    handoff(RING, RU + RV)

    def load_x(t):
        P.dma("sp", lambda e: e.dma_start(out=xt[t % 2][:], in_=x_d[t * 128:(t + 1) * 128, :]),
              f"ld_xt{t % 2}", writes=[XT[t % 2]])

    def rms_stats(src, SRC, k, junk, JUNK):
        ssq = stat[:, k:k + 1]
        P.op("act", act_fn(junk, src, AF.Square, accum=ssq), reads=[SRC], writes=[JUNK, STAT[k]])
        P.op("act", act_fn(ssq, ssq, AF.Sqrt, bias=EPS, scale=1.0 / D), reads=[STAT[k]], writes=[STAT[k]])
        P.op("dve", lambda e: e.reciprocal(out=ssq, in_=ssq), reads=[STAT[k]], writes=[STAT[k]])
        return ssq

    def head_norm(y, YB, gcol, yn_off, YN, bk=2):
        P.op("act", act_fn(sq, y, AF.Square), reads=YB, writes=[SQ])
        P.op("pe", lambda e: e.matmul(ps[:, bk, :], lhsT=blockones[:], rhs=sq, start=True, stop=True),
             reads=[SQ, CONSTS], writes=[PB[bk]])
        P.op("act", act_fn(rstd, ps[:, bk, :], AF.Sqrt, bias=EPS), reads=[PB[bk]], writes=[RSTD])
        P.op("dve", lambda e: e.reciprocal(out=rstd, in_=rstd), reads=[RSTD], writes=[RSTD])

        def f(e):
            for c in range(4):
                r_ = e.scalar_tensor_tensor(out=ynT[:, yn_off + c, :], in0=y[:, c * 128:(c + 1) * 128],
                                            scalar=vecs[:, gcol + c:gcol + c + 1],
                                            in1=rstd[:, c * 128:(c + 1) * 128], op0=ALU.mult, op1=ALU.mult)
            return r_
        P.op("dve", f, reads=list(YB) + [RSTD, VEC], writes=[YN])

    first_w = [True]
    gather_ctr = [0]
    first_gather = [True]

    def finish(t):
        return P.dma("sp", lambda e: e.dma_start(out=out_d[t * 128:(t + 1) * 128, :], in_=ot[:]), "st_out", reads=[OT])

    ZB = [1, 2, 3, 4, 7]

    def pre(t):
        X = XT[t % 2]
        xtt = xt[t % 2]
        x1 = x1s[t % 2]; X1 = X1s[t % 2]
        h2b = h2bs[t % 2]; H2B = H2Bs[t % 2]
        idx32 = idx32s[t % 2]; IDX = IDXs[t % 2]
        gsm = gsms[t % 2]; GSM = GSMs[t % 2]
        load_x(t)
        P.handoff(PEERB, MIXB)
        r1 = rms_stats(xtt[:], X, 0, ot[:], OT)
        P.op("dve", lambda e: e.scalar_tensor_tensor(out=ot[:], in0=xtt[:], scalar=r1, in1=gmod1_bc[:],
                                                     op0=ALU.mult, op1=ALU.mult),
             reads=[X, STAT[0], CONSTS], writes=[OT])
        P.op("dve", lambda e: e.tensor_tensor(out=h[:], in0=ot[:], in1=sh1_bc[:], op=ALU.add),
             reads=[OT, CONSTS], writes=[H])
        if t == dbg_tile:
            dump("h", h[:], [128, D], BF16, H)

        def trh(e, src=h):
            for c in range(8):
                r_ = e.transpose(out=trb[:, c, :], in_=src[:, c * 128:(c + 1) * 128], identity=ident_bf[:])
            return r_
        P.op("pe", trh, reads=[H, CONSTS], writes=[PB[0]])
        P.op("act", lambda e: e.copy(out=hT[:].rearrange("p a b -> p (a b)"), in_=ps[:, 0, :].bitcast(BF16)),
             reads=[PB[0]], writes=[HT])
        for bk in range(5):
            def zmm(e, bk=bk):
                for f4 in range(4):
                    fc = bk * 4 + f4
                    for kc in range(8):
                        r_ = e.matmul(ps[:, ZB[bk], f4 * 128:(f4 + 1) * 128],
                                      lhsT=w_in_bf[:, kc, fc * 128:(fc + 1) * 128], rhs=hT[:, kc, :],
                                      start=(kc == 0), stop=(kc == 7))
                return r_
            P.op("pe", zmm, reads=[HT, WIN], writes=[PB[ZB[bk]]])

        P.op("act", lambda e: e.copy(out=gc_sb, in_=ps[:, 2, :]), reads=[PB[2]], writes=[GC])
        P.op("dve", lambda e: e.tensor_tensor(out=mbuf[:, :, 2:130],
                                              in0=gc_sb.rearrange("p (c t) -> p c t", c=4),
                                              in1=ps[:, 3, :].rearrange("p (c t) -> p c t", c=4), op=ALU.mult),
             reads=[GC, PB[3]], writes=[MBUF])
        for k in (2, 1, 0):
            for c in range(4):
                wcol = vecs[:, V_WA + 4 * k + c:V_WA + 4 * k + c + 1]
                dstc = ca[:, c * 128:(c + 1) * 128]
                if k == 2:
                    P.op("dve", lambda e, c=c, wcol=wcol, dstc=dstc: e.tensor_scalar(
                        out=dstc, in0=mbuf[:, c, 2:130], scalar1=wcol, scalar2=None, op0=ALU.mult),
                        reads=[MBUF, VEC], writes=[CA[c]])
                else:
                    P.op("dve", lambda e, c=c, k=k, wcol=wcol, dstc=dstc: e.scalar_tensor_tensor(
                        out=dstc, in0=mbuf[:, c, k:k + 128], scalar=wcol, in1=dstc, op0=ALU.mult, op1=ALU.add),
                        reads=[MBUF, VEC, CA[c]], writes=[CA[c]])
        P.op("dve", lambda e: e.tensor_copy(out=mbuf[:, :, 0:2], in_=mbuf[:, :, 128:130]), reads=[MBUF], writes=[MBUF])
        P.op("dve", lambda e: e.tensor_tensor(out=ca, in0=ps[:, 1, :], in1=ca, op=ALU.mult),
             reads=[PB[1]] + CA, writes=CA)
        head_norm(ca, CA, V_GNA, 0, YNT[0])

        P.op("act", lambda e: e.copy(out=xrbuf[:, :, 3:131], in_=ps[:, 4, :].rearrange("p (c t) -> p c t", c=4)),
             reads=[PB[4]], writes=[XRBUF])
        for k in (3, 2, 1, 0):
            for c in range(4):
                wcol = vecs[:, V_WB + 4 * k + c:V_WB + 4 * k + c + 1]
                dstc = xc[:, c * 128:(c + 1) * 128]
                if k == 3:
                    bcol = vecs[:, V_CBB + c:V_CBB + c + 1]
                    P.op("dve", lambda e, c=c, wcol=wcol, bcol=bcol, dstc=dstc: e.tensor_scalar(
                        out=dstc, in0=xrbuf[:, c, 3:131], scalar1=wcol, scalar2=bcol, op0=ALU.mult, op1=ALU.add),
                        reads=[XRBUF, VEC], writes=[XC[c]])
                else:
                    P.op("dve", lambda e, c=c, k=k, wcol=wcol, dstc=dstc: e.scalar_tensor_tensor(
                        out=dstc, in0=xrbuf[:, c, k:k + 128], scalar=wcol, in1=dstc, op0=ALU.mult, op1=ALU.add),
                        reads=[XRBUF, VEC, XC[c]], writes=[XC[c]])
        P.op("dve", lambda e: e.tensor_copy(out=xrbuf[:, :, 0:3], in_=xrbuf[:, :, 128:131]), reads=[XRBUF], writes=[XRBUF])
        P.op("act", lambda e: e.copy(out=xcb[:].rearrange("p a b -> p (a b)"), in_=xc), reads=XC, writes=[XCB])

        def gate_mm(e, wbf, bank):
            for c in range(4):
                r_ = e.matmul(ps[:, bank, c * 128:(c + 1) * 128], lhsT=wbf[:, c, :], rhs=xcb[:, c, :], start=True, stop=True)
            return r_
        P.op("pe", lambda e: gate_mm(e, wr_bf, 3), reads=[XCB, CONSTS], writes=[PB[3]])
        P.op("pe", lambda e: gate_mm(e, wi_bf, 4), reads=[XCB, CONSTS], writes=[PB[4]])

        def sig(e, dst, bank, bcol0):
            for c in range(4):
                r_ = e.activation(dst[:, c * 128:(c + 1) * 128], ps[:, bank, c * 128:(c + 1) * 128], AF.Sigmoid,
                                  bias=vecs[:, bcol0 + c:bcol0 + c + 1])
            return r_
        P.op("act", lambda e: sig(e, rr, 3, V_BR), reads=[PB[3], VEC], writes=[RR])
        P.op("act", lambda e: sig(e, ii_, 4, V_BI), reads=[PB[4], VEC], writes=[II])

        def expa(e, dst, col0):
            for c in range(4):
                r_ = e.activation(dst[:, c * 128:(c + 1) * 128], rr[:, c * 128:(c + 1) * 128], AF.Exp,
                                  scale=nsp[:, col0 + c:col0 + c + 1])
            return r_
        P.op("act", lambda e: expa(e, aa, 0), reads=[RR, CONSTS], writes=[AA])
        P.op("act", lambda e: expa(e, rr, 4), reads=[RR, CONSTS], writes=[RR])
        P.op("act", act_fn(rr, rr, AF.Sqrt, bias=1.0, scale=-1.0), reads=[RR], writes=[RR])
        P.op("dve", lambda e: e.tensor_tensor(out=ii_, in0=ii_, in1=xc, op=ALU.mult), reads=[II] + XC, writes=[II])
        P.op("dve", lambda e: e.tensor_tensor(out=rr, in0=rr, in1=ii_, op=ALU.mult), reads=[RR, II], writes=[RR])

        def scan(e):
            for c in range(4):
                r_ = e.tensor_tensor_scan(out=hs[:, c * 128:(c + 1) * 128], data0=aa[:, c * 128:(c + 1) * 128],
                                          data1=rr[:, c * 128:(c + 1) * 128], initial=hstate[:, c:c + 1],
                                          op0=ALU.mult, op1=ALU.add)
            return r_
        P.op("dve", scan, reads=[AA, RR, HST], writes=[HS])
        P.op("dve", lambda e: e.tensor_copy(out=hstate[:].unsqueeze(2),
                                            in_=hs.rearrange("p (c t) -> p c t", c=4)[:, :, 127:128]),
             reads=[HS], writes=[HST])
        P.op("act", act_fn(gg, ps[:, 7, :], AF.Gelu), reads=[PB[7]], writes=[GG])
        P.op("dve", lambda e: e.tensor_tensor(out=hs, in0=hs, in1=gg, op=ALU.mult), reads=[HS, GG], writes=[HS])
        head_norm(hs, [HS], V_GNB, 4, YNT[1])
        if t == dbg_tile:
            dump("ynT", ynT[:], [128, 8, 128], BF16, YNT[1])

        def omm(e):
            for half in range(2):
                for kc in range(8):
                    r_ = e.matmul(ps[:, 1 + half, :], lhsT=ynT[:, kc, :], rhs=w_out_bf[:, kc, half * 512:(half + 1) * 512],
                                  start=(kc == 0), stop=(kc == 7))
            return r_
        P.op("pe", omm, reads=YNT + [WOUT], writes=[PB[1], PB[2]])
        P.op("dve", lambda e: e.tensor_tensor(out=ot[:].rearrange("p (a b) -> p a b", a=2), in0=ps[:, 1:3, :],
                                              in1=g1_bc[:].rearrange("p (a b) -> p a b", a=2), op=ALU.mult),
             reads=[PB[1], PB[2], CONSTS], writes=[OT])
        P.op("dve", lambda e: e.tensor_tensor(out=x1[:], in0=ot[:], in1=xtt[:], op=ALU.add),
             reads=[OT, X], writes=[X1])
        if t == dbg_tile:
            dump("x1", x1[:], [128, D], F32, X1)

        r2 = rms_stats(x1[:], X1, 1, ot[:], OT)
        P.op("dve", lambda e: e.scalar_tensor_tensor(out=ot[:], in0=x1[:], scalar=r2, in1=gmod2_bc[:],
                                                     op0=ALU.mult, op1=ALU.mult),
             reads=[X1, STAT[1], CONSTS], writes=[OT])
        P.op("dve", lambda e: e.tensor_tensor(out=h2b[:], in0=ot[:], in1=sh2_bc[:], op=ALU.add),
             reads=[OT, CONSTS], writes=[H2B])
        P.op("pe", lambda e: trh(e, h2b), reads=[H2B, CONSTS], writes=[PB[0]])
        P.op("act", lambda e: e.copy(out=h2T[:].rearrange("p a b -> p (a b)"), in_=ps[:, 0, :].bitcast(BF16)),
             reads=[PB[0]], writes=[H2T])
        P.handoff(MIXB, PEERB)
        for bk in range(2):
            def qmm(e, bk=bk):
                for f4 in range(4):
                    hd = bk * 4 + f4
                    for kc in range(8):
                        r_ = e.matmul(ps[:, 3 + bk, f4 * 128:(f4 + 1) * 128],
                                      lhsT=w_q_bf[:, kc, hd * 128:(hd + 1) * 128], rhs=h2T[:, kc, :],
                                      start=(kc == 0), stop=(kc == 7))
                return r_
            P.op("pe", qmm, reads=[H2T, WQ], writes=[PB[3 + bk]])
            P.op("act", lambda e, bk=bk: e.copy(out=qT[:, bk * 512:(bk + 1) * 512], in_=ps[:, 3 + bk, :]),
                 reads=[PB[3 + bk]], writes=[QT])
        qT3 = qT.rearrange("p (a b) -> p a b", a=8)
        for pr in range(4):
            bank = 1 + pr % 2

            def smm(e, pr=pr, bank=bank):
                for hh in range(2):
                    hd = 2 * pr + hh
                    for c in range(2):
                        col = (hh * 2 + c) * 128
                        r_ = e.matmul(ps[:, bank, col:col + 128], lhsT=qT3[:, hd, :],
                                      rhs=skT[:, c, hd, :], start=True, stop=True)
                return r_
            P.op("pe", smm, reads=[QT, SKT], writes=[PB[bank]])
            for hh in range(2):
                hd = 2 * pr + hh
                for c in range(2):
                    col = (hh * 2 + c) * 128
                    src = ps[:, bank, col:col + 128]
                    wi_ = (hd * 2 + c) % 2
                    wk = work[:, wi_, :]
                    TB = TOP[hd][c]

                    def f1(e, src=src, hd=hd, c=c):
                        return e.max(out=top_s[:, hd, c, 0:8], in_=src)
                    P.op("dve", f1, reads=[PB[bank]], writes=[TB])

                    def f2(e, src=src, hd=hd, c=c, wk=wk):
                        return e.match_replace(out=wk, in_to_replace=top_s[:, hd, c, 0:8], in_values=src, imm_value=NEG)
                    P.op("dve", f2, reads=[PB[bank], TB], writes=[WORK[wi_]])

                    def f3(e, src=src, hd=hd, c=c):
                        return e.max_index(out=top_i[:, hd, c, 0:8], in_max=top_s[:, hd, c, 0:8], in_values=src)
                    P.op("dve", f3, reads=[PB[bank], TB], writes=[TB])

                    def f4_(e, hd=hd, c=c, wk=wk):
                        return e.max(out=top_s[:, hd, c, 8:16], in_=wk)
                    P.op("dve", f4_, reads=[WORK[wi_], TB], writes=[TB])

                    def f5(e, hd=hd, c=c, wk=wk):
                        return e.max_index(out=top_i[:, hd, c, 8:16], in_max=top_s[:, hd, c, 8:16], in_values=wk)
                    P.op("dve", f5, reads=[WORK[wi_], TB], writes=[TB])
        ALLTOP = [TOP[hd][c] for hd in range(8) for c in range(2)]
        P.op("dve", lambda e: e.tensor_copy(out=top_if[:], in_=top_i[:]), reads=ALLTOP, writes=[TOPIF])
        cand4 = cand.rearrange("p (a b c) -> p a b c", a=8, b=16)
        P.op("dve", lambda e: e.tensor_tensor(
            out=cand4, in0=top_s[:, :, 0, :].unsqueeze(3).broadcast_to([128, 8, 16, 16]),
            in1=top_s[:, :, 1, :].unsqueeze(2).broadcast_to([128, 8, 16, 16]), op=ALU.add),
            reads=ALLTOP, writes=[CAND])
        for hd in range(8):
            ch = cand[:, hd * 256:(hd + 1) * 256]
            w2 = work2[:, hd % 2, :]
            W2 = WORK2[hd % 2]
            BB = BEST[hd]
            P.op("dve", lambda e, hd=hd, ch=ch: e.max(out=best_s[:, hd, 0:8], in_=ch), reads=[CAND], writes=[BB])
            P.op("dve", lambda e, hd=hd, ch=ch, w2=w2: e.match_replace(out=w2, in_to_replace=best_s[:, hd, 0:8],
                                                                      in_values=ch, imm_value=NEG),
                 reads=[CAND, BB], writes=[W2])
            P.op("dve", lambda e, hd=hd, ch=ch: e.max_index(out=best_pos[:, hd, 0:8], in_max=best_s[:, hd, 0:8], in_values=ch),
                 reads=[CAND, BB], writes=[BB])
            P.op("dve", lambda e, hd=hd, w2=w2: e.max(out=best_s[:, hd, 8:16], in_=w2), reads=[W2, BB], writes=[BB])
            P.op("dve", lambda e, hd=hd, w2=w2: e.max_index(out=best_pos[:, hd, 8:16], in_max=best_s[:, hd, 8:16], in_values=w2),
                 reads=[W2, BB], writes=[BB])
        bp = best_pos[:].rearrange("p a b -> p (a b)")
        P.op("dve", lambda e: e.tensor_single_scalar(out=iju[:, 0, :], in_=bp, scalar=4, op=ALU.logical_shift_right),
             reads=BEST, writes=[IJU])
        P.op("dve", lambda e: e.tensor_single_scalar(out=iju[:, 1, :], in_=bp, scalar=15, op=ALU.bitwise_and),
             reads=BEST + [IJU], writes=[IJU])
        P.op("dve", lambda e: e.tensor_copy(out=ijf[:], in_=iju[:]), reads=[IJU], writes=[IJF])
        oh3 = oh.rearrange("p (a b) -> p a b", b=16)
        oh4 = oh.rearrange("p (a b c) -> p a b c", a=8, b=16)
        pr4 = prod.rearrange("p (a b c) -> p a b c", a=8, b=16)
        pr3 = prod.rearrange("p (a b) -> p a b", b=16)
        for w in range(2):
            P.op("dve", lambda e, w=w: e.tensor_tensor(
                out=oh3, in0=ijf[:, w, :].unsqueeze(2).broadcast_to([128, 128, 16]),
                in1=iota16[:].unsqueeze(1).broadcast_to([128, 128, 16]), op=ALU.is_equal),
                reads=[IJF, CONSTS], writes=[OH])
            P.op("dve", lambda e, w=w: e.tensor_tensor(
                out=pr4, in0=oh4, in1=top_if[:, :, w, :].unsqueeze(2).broadcast_to([128, 8, 16, 16]), op=ALU.mult),
                reads=[OH, TOPIF], writes=[PROD])
            P.op("dve", lambda e, w=w: e.tensor_reduce(out=i12[:, w, :], in_=pr3, axis=AX.X, op=ALU.add),
                 reads=[PROD], writes=[I12])
        P.op("dve", lambda e: e.scalar_tensor_tensor(out=idxf[:], in0=i12[:, 0, :], scalar=128.0, in1=i12[:, 1, :],
                                                     op0=ALU.mult, op1=ALU.add), reads=[I12], writes=[IDXF])
        P.op("dve", lambda e: e.tensor_copy(out=idx32[:], in_=idxf[:]), reads=[IDXF], writes=[IDX])
        P.op("dve", lambda e: e.tensor_tensor(out=ev_[:], in0=best_s[:], in1=best_s[:, :, 0:1].broadcast_to([128, 8, 16]),
                                              op=ALU.subtract), reads=BEST, writes=[ESM])
        P.op("act", act_fn(ev_[:], ev_[:], AF.Exp), reads=[ESM], writes=[ESM])
        P.op("dve", lambda e: e.tensor_reduce(out=ssum[:, 0:8], in_=ev_[:], axis=AX.X, op=ALU.add), reads=[ESM], writes=[SSUM])
        P.op("dve", lambda e: e.reciprocal(out=ssum[:, 0:8], in_=ssum[:, 0:8]), reads=[SSUM], writes=[SSUM])
        P.op("dve", lambda e: e.tensor_tensor(out=gsm[:], in0=ev_[:], in1=ssum[:, 0:8].unsqueeze(2).broadcast_to([128, 8, 16]),
                                              op=ALU.mult), reads=[ESM, SSUM], writes=[GSM])
        if t == dbg_tile:
            dump("h2b", h2b[:], [128, D], BF16, H2B)
            dump("best_s", best_s[:], [128, 8, 16], F32, BEST[7])
            dump("idx32", idx32[:], [128, 128], I32, IDX)
            dump("gsm", gsm[:], [128, 8, 16], F32, GSM)


    def post(t, recs):
        x1 = x1s[t % 2]; X1 = X1s[t % 2]
        h2b = h2bs[t % 2]; H2B = H2Bs[t % 2]
        idx32 = idx32s[t % 2]; IDX = IDXs[t % 2]
        gsm = gsms[t % 2]; GSM = GSMs[t % 2]
        gflat = gsm[:].rearrange("p a b -> p (a b)")
        ri = 0
        slots_of = schedule(recs, gap=GAP)
        order = sorted(range(len(recs)), key=lambda i: (slots_of[i], i))
        if t == 0 and recs:
            print("pre-phase schedule span", max(slots_of), "records", len(recs))
        def tail(s):
            SB_ = SLOTB[s % NSL]
            jv = jvs[s]
            P.op("act", act_fn(coefv[:, s:s + 1], gev[:, s:s + 1], AF.Copy, scale=gflat[:, s:s + 1]),
                 reads=[SB_, GSM], writes=[SB_])
            dj = s % 4
            P.op("act", act_fn(diag[:, dj, :], ident_bf[:], AF.Copy, scale=coefv[:, s:s + 1]),
                 reads=[SB_, CONSTS], writes=[DIAG[dj]])

            def vmm(e, jv=jv, s=s, dj=dj):
                for half in range(2):
                    r_ = e.matmul(ps[:, 5 + half, :], lhsT=diag[:, dj, :],
                                  rhs=rv(jv)[:, half * 512:(half + 1) * 512],
                                  start=(s == 0), stop=(s == 127))
                return r_
            P.op("pe", vmm, reads=[DIAG[dj], RV[jv]], writes=[PB[5], PB[6]], skip_self=(s > 0))

        jvs = {}
        for s in range(128):
            g_ = gather_ctr[0]
            gather_ctr[0] += 1
            ju, jv = g_ % NU, g_ % NVR
            jvs[s] = jv
            extra = uv_tokens if first_gather[0] else ()
            first_gather[0] = False
            P.dma("pool", lambda e, ju=ju, jv=jv, s=s: e.indirect_dma_start(
                out=ruv(ju, jv), out_offset=None, in_=uv_d[:, :],
                in_offset=bass.IndirectOffsetOnAxis(ap=idx32[:, s:s + 1], axis=0)),
                f"gth{ju}", reads=[IDX], writes=[RU[ju], RV[jv]], extra=extra)
            SB_ = SLOTB[s % NSL]
            P.op("dve", lambda e, ju=ju, s=s: e.scalar_tensor_tensor(
                out=junkD[:], in0=ru(ju), scalar=1.0, in1=h2b[:], op0=ALU.mult, op1=ALU.mult,
                accum_out=actv[:, s:s + 1]), reads=[RU[ju], H2B], writes=[JUNKD, SB_])
            P.op("act", act_fn(gev[:, s:s + 1], actv[:, s:s + 1], AF.Gelu), reads=[SB_], writes=[SB_])
            if s >= SKEW:
                tail(s - SKEW)
            while ri < len(order) and slots_of[order[ri]] <= s:
                P.commit(recs[order[ri]])
                ri += 1
        for s in range(128 - SKEW, 128):
            tail(s)
        if t == dbg_tile:
            dump("actv", actv[:], [128, 128], F32, SLOTB[3])

        while ri < len(order):
            P.commit(recs[order[ri]])
            ri += 1
        P.op("dve", lambda e: e.tensor_tensor(out=ot[:].rearrange("p (a b) -> p a b", a=2), in0=ps[:, 5:7, :],
                                              in1=g2_bc[:].rearrange("p (a b) -> p a b", a=2), op=ALU.mult),
             reads=[PB[5], PB[6], CONSTS], writes=[OT])
        P.op("dve", lambda e: e.tensor_tensor(out=ot[:], in0=ot[:], in1=x1[:], op=ALU.add), reads=[OT, X1], writes=[OT])
        r3 = rms_stats(ot[:], OT, 2, h[:], H)
        P.op("dve", lambda e: e.scalar_tensor_tensor(out=ot[:], in0=ot[:], scalar=r3, in1=fg_bc[:],
                                                     op0=ALU.mult, op1=ALU.mult),
             reads=[OT, STAT[2], CONSTS], writes=[OT])
        return P.dma("sp", lambda e: e.dma_start(out=out_d[t * 128:(t + 1) * 128, :], in_=ot[:]), "st_out", reads=[OT])

    pre(0)
    last = None
    for t in range(NT):
        recs = []
        if t + 1 < NT:
            P.defer = []
            pre(t + 1)
            recs = P.defer
            P.defer = None
        last = post(t, recs)
    fin = [last] + [(k, v) for k, v in P.dsem.items() if k.startswith("dbg_")]
    P.wait_only("sp", fin)

    sems = {}
    for e in ENG:
        sems[e] = es.enter_context(nc.semaphore("pc_" + e))
    for k in P.dsem:
        sems[k] = es.enter_context(nc.semaphore("d_" + k))
    blockname = {"pe": "tensor", "act": "scalar", "dve": "vector", "pool": "gpsimd", "sp": "sync"}
    with nc.Block() as block:
        for e in ENG:
            def body(eng, e=e):
                for waits, fn, inc in P.ops[e]:
                    for k, v in waits:
                        eng.wait_ge(sems[k], v)
                    if fn is None:
                        continue
                    inst = fn(eng)
                    inst.then_inc(sems[inc[0]], inc[1])
            getattr(block, blockname[e])(body)
    es.close()
    return nc, list(dbg_out.keys())


def _kc_layout(w):
    n = w.shape[1]
    return np.ascontiguousarray(w.reshape(8, 128, n).transpose(1, 0, 2))


def _col4(v):
    return np.ascontiguousarray(v.reshape(4, 128).T)


def _blockdiag(w):
    o = np.zeros((128, 4, 128), np.float32)
    for hd in range(8):
        c, q = hd // 2, hd % 2
        o[q * 64:(q + 1) * 64, c, q * 64:(q + 1) * 64] = w[hd]
    return o


def prep_shared(inp):
    f = lambda a: np.asarray(a, dtype=np.float32)
    vecs = np.zeros((128, NV), np.float32)
    caw = f(inp["conv_a_w"])[0]
    cbw = f(inp["conv_b_w"])[0]
    for k in range(3):
        vecs[:, V_WA + 4 * k:V_WA + 4 * k + 4] = _col4(caw[k])
    for k in range(4):
        vecs[:, V_WB + 4 * k:V_WB + 4 * k + 4] = _col4(cbw[k])
    vecs[:, V_CBB:V_CBB + 4] = _col4(f(inp["conv_b_b"])[0])
    vecs[:, V_BR:V_BR + 4] = _col4(f(inp["b_r"])[0].reshape(512))
    vecs[:, V_BI:V_BI + 4] = _col4(f(inp["b_i"])[0].reshape(512))
    vecs[:, V_LAM:V_LAM + 4] = _col4(f(inp["lru_lambda"])[0].reshape(512))
    vecs[:, V_GNA:V_GNA + 4] = _col4(f(inp["gn_a"])[0])
    vecs[:, V_GNB:V_GNB + 4] = _col4(f(inp["gn_b"])[0])
    sk = f(inp["sub_keys"])[0]
    skT1 = sk.transpose(1, 3, 0, 2).reshape(128, 8, 128)
    skT = np.zeros((128, 2, 8, 128), np.float32)
    skT[0:64, 0] = skT1[0:64]
    skT[64:128, 1] = skT1[64:128]
    grow = np.concatenate([f(inp["norm1_g"])[0], f(inp["norm2_g"])[0], f(inp["final_g"])])[None, :]
    return {
        "w_ada": _kc_layout(f(inp["w_ada"])[0]),
        "b_ada": np.ascontiguousarray(f(inp["b_ada"])[0][None, :]),
        "grow": np.ascontiguousarray(grow),
        "w_in": _kc_layout(f(inp["w_in"])[0]),
        "w_out": _kc_layout(f(inp["w_out"])[0]),
        "w_q": _kc_layout(f(inp["w_q"])[0]),
        "vecs": vecs,
        "wr_bd": _blockdiag(f(inp["w_r"])[0]),
        "wi_bd": _blockdiag(f(inp["w_i"])[0]),
        "skT": skT,
        "expert_u": np.ascontiguousarray(f(inp["expert_u"])[0]),
        "expert_v": np.ascontiguousarray(f(inp["expert_v"])[0]),
    }


def kernel(**inputs):
    x = np.asarray(inputs["x"], dtype=np.float32)
    c = np.asarray(inputs["c"], dtype=np.float32)
    B, S, _ = x.shape
    shared = prep_shared(inputs)
    nc, _ = build(S // 128)
    in_maps = []
    for b in range(B):
        m = dict(shared)
        m["x"] = np.ascontiguousarray(x[b])
        m["cin"] = np.ascontiguousarray(c[b].reshape(8, 128).T)
        in_maps.append(m)
    res = run_bass_kernel_spmd(nc, in_maps, core_ids=list(range(B)))
    return np.stack([np.asarray(r["out"]) for r in res.results], axis=0).astype(np.float32)
```
